# Optimizing a Trainium2 kernel written in Bass

```python
import math
import jax
import jax.numpy as jnp
from jax import lax
import numpy as np

D_MODEL = 2048
BATCH = 2
SEQ = 16384
DEPTH = 2

GRID_W = 64
CTX_LEN = 256
ROPE_THETA = 10000.0
Q_BLOCK = 128
NORM_EPS = 1e-6
NEG_INF = -1e30

N_BRANCHES = 4
BRANCH_WIDTH = D_MODEL // 4
MLSTM_HEADS = 4
MLSTM_DV = BRANCH_WIDTH // MLSTM_HEADS
MLSTM_DK = MLSTM_DV // 2
MLSTM_CHUNK = 128
DIFF_HEADS = 4
DIFF_D = BRANCH_WIDTH // (2 * DIFF_HEADS)
SWA_HEADS = 8
SWA_KV_HEADS = 2
SWA_D = BRANCH_WIDTH // SWA_HEADS
SWA_WINDOW = 128
GQA_HEADS = 4
GQA_KV_HEADS = 2
GQA_D = BRANCH_WIDTH // GQA_HEADS
FFN_HIDDEN = -(-8 * D_MODEL // (3 * 256)) * 256

IN_SPLITS = (
    MLSTM_HEADS * MLSTM_DK,
    MLSTM_HEADS * MLSTM_DK,
    MLSTM_HEADS * MLSTM_DV,
    MLSTM_HEADS * MLSTM_DV,
    4 * MLSTM_HEADS,
    DIFF_HEADS * 2 * DIFF_D,
    DIFF_HEADS * 2 * DIFF_D,
    DIFF_HEADS * 2 * DIFF_D,
    SWA_HEADS * SWA_D,
    SWA_KV_HEADS * SWA_D,
    SWA_KV_HEADS * SWA_D,
    GQA_HEADS * GQA_D,
    GQA_KV_HEADS * GQA_D,
    GQA_KV_HEADS * GQA_D,
    N_BRANCHES * D_MODEL,
)
D_IN = sum(IN_SPLITS)

kernel_name = "hybrid_mlstm_diffattn_swa_gqa_dit_block"


def rms_norm(x, g):
    xf = x.astype(jnp.float32)
    y = xf * lax.rsqrt(jnp.mean(xf * xf, axis=-1, keepdims=True) + NORM_EPS)
    return (y * g.astype(jnp.float32)).astype(x.dtype)


def modulate(h, shift, scale):
    return h * (1 + scale) + shift


def split_cols(p):
    offsets = []
    acc = 0
    for s in IN_SPLITS[:-1]:
        acc += s
        offsets.append(acc)
    return jnp.split(p, offsets, axis=-1)


def axial_rope_tables(rows, head_dim):
    row = jnp.repeat(jnp.arange(rows, dtype=jnp.float32), GRID_W)
    col = jnp.tile(jnp.arange(GRID_W, dtype=jnp.float32), rows)
    n_freq = head_dim // 4
    inv = ROPE_THETA ** (-jnp.arange(n_freq, dtype=jnp.float32) / n_freq)
    ang = jnp.stack([row[:, None] * inv, col[:, None] * inv], axis=1)
    return jnp.cos(ang), jnp.sin(ang)


def apply_rope(x, cos, sin):
    b, n, h, d = x.shape
    xr = x.reshape(b, n, h, 2, 2, d // 4)
    x1, x2 = xr[..., 0, :], xr[..., 1, :]
    c = cos[None, :, None].astype(x.dtype)
    s = sin[None, :, None].astype(x.dtype)
    return jnp.stack([x1 * c - x2 * s, x2 * c + x1 * s], axis=-2).reshape(b, n, h, d)


def dense_gqa(q, k, v, sink=None):
    b, nq, hq, d = q.shape
    hkv, dv = k.shape[2], v.shape[-1]
    g = hq // hkv
    nb = nq // Q_BLOCK
    qb = q.reshape(b, nb, Q_BLOCK, hkv, g, d).transpose(1, 0, 2, 3, 4, 5)
    scale = d ** -0.5

    def block(qblk):
        s = jnp.einsum("bqhgd,bkhd->bhgqk", qblk, k).astype(jnp.float32) * scale
        if sink is not None:
            sk = jnp.broadcast_to(sink.astype(jnp.float32).reshape(1, hkv, g, 1, 1), s.shape[:-1] + (1,))
            p = jax.nn.softmax(jnp.concatenate([s, sk], axis=-1), axis=-1)[..., :-1]
        else:
            p = jax.nn.softmax(s, axis=-1)
        return jnp.einsum("bhgqk,bkhd->bqhgd", p.astype(v.dtype), v)

    o = lax.map(block, qb)
    return o.transpose(1, 0, 2, 3, 4, 5).reshape(b, nq, hq, dv)


def window_attention(q, k, v, k_ctx, v_ctx, sink):
    b, n, hq, d = q.shape
    hkv = k.shape[2]
    g = hq // hkv
    nb = n // Q_BLOCK
    nc = k_ctx.shape[1]
    span = Q_BLOCK + 2 * SWA_WINDOW
    pad = ((0, 0), (SWA_WINDOW, SWA_WINDOW), (0, 0), (0, 0))
    k_pad = jnp.pad(k, pad)
    v_pad = jnp.pad(v, pad)
    qb = q.reshape(b, nb, Q_BLOCK, hkv, g, d).transpose(1, 0, 2, 3, 4, 5)
    scale = d ** -0.5
    sink_l = sink.astype(jnp.float32).reshape(1, hkv, g, 1, 1)

    def block(args):
        j, qblk = args
        start = j * Q_BLOCK
        kb = lax.dynamic_slice_in_dim(k_pad, start, span, axis=1)
        vb = lax.dynamic_slice_in_dim(v_pad, start, span, axis=1)
        qpos = start + jnp.arange(Q_BLOCK)
        kpos = start - SWA_WINDOW + jnp.arange(span)
        valid = (jnp.abs(qpos[:, None] - kpos[None, :]) <= SWA_WINDOW) & (kpos[None, :] >= 0) & (kpos[None, :] < n)
        s_band = jnp.einsum("bqhgd,bkhd->bhgqk", qblk, kb).astype(jnp.float32) * scale
        s_band = jnp.where(valid, s_band, NEG_INF)
        s_ctx = jnp.einsum("bqhgd,bkhd->bhgqk", qblk, k_ctx).astype(jnp.float32) * scale
        s_sink = jnp.broadcast_to(sink_l, s_ctx.shape[:-1] + (1,))
        p = jax.nn.softmax(jnp.concatenate([s_ctx, s_band, s_sink], axis=-1), axis=-1).astype(v.dtype)
        return (jnp.einsum("bhgqk,bkhd->bqhgd", p[..., :nc], v_ctx)
                + jnp.einsum("bhgqk,bkhd->bqhgd", p[..., nc:nc + span], vb))

    o = lax.map(block, (jnp.arange(nb), qb))
    return o.transpose(1, 0, 2, 3, 4, 5).reshape(b, n, hq, v.shape[-1])


def diff_attention(q, k, v, lam):
    b, nq, h, _, d = q.shape
    nb = nq // Q_BLOCK
    qb = q.reshape(b, nb, Q_BLOCK, h, 2, d).transpose(1, 0, 2, 3, 4, 5)
    scale = d ** -0.5
    lam_f = lam.astype(jnp.float32)[None, :, None, None]

    def block(qblk):
        s = jnp.einsum("bqhmd,bkhmd->bhmqk", qblk, k).astype(jnp.float32) * scale
        p = jax.nn.softmax(s, axis=-1)
        a = p[:, :, 0] - lam_f * p[:, :, 1]
        return jnp.einsum("bhqk,bkhe->bqhe", a.astype(v.dtype), v)

    o = lax.map(block, qb)
    return o.transpose(1, 0, 2, 3, 4).reshape(b, nq, h, v.shape[-1])


def mlstm_chunked(q, k, v, ig, lf, state):
    b, n, h, dk = q.shape
    dv = v.shape[-1]
    L = MLSTM_CHUNK
    nc = n // L

    def to_chunks(a):
        a = a.astype(jnp.float32).reshape((b, nc, L, h) + a.shape[3:])
        return jnp.moveaxis(a, 2, 3).swapaxes(0, 1)

    xs = (to_chunks(q), to_chunks(k) * (dk ** -0.5), to_chunks(v), to_chunks(ig), to_chunks(lf))
    tril = jnp.tril(jnp.ones((L, L), dtype=bool))

    def step(carry, inp):
        C, nv, m = carry
        qc, kc, vc, ic, fc = inp
        bcum = jnp.cumsum(fc, axis=-1)
        dmat = bcum[..., :, None] - bcum[..., None, :] + ic[..., None, :]
        dmat = jnp.where(tril, dmat, NEG_INF)
        inter = bcum + m[..., None]
        m_t = jnp.maximum(inter, jnp.max(dmat, axis=-1))
        w_intra = jnp.exp(dmat - m_t[..., None])
        w_inter = jnp.exp(inter - m_t)
        s = jnp.einsum("bhtd,bhsd->bhts", qc, kc) * w_intra
        num = (w_inter[..., None] * jnp.einsum("bhvd,bhtd->bhtv", C, qc)
               + jnp.einsum("bhts,bhsv->bhtv", s, vc))
        den = w_inter * jnp.einsum("bhd,bhtd->bht", nv, qc) + jnp.sum(s, axis=-1)
        hout = num / jnp.maximum(jnp.abs(den), jnp.exp(-m_t))[..., None]
        b_last = bcum[..., -1]
        log_w = b_last[..., None] - bcum + ic
        m_new = jnp.maximum(b_last + m, jnp.max(log_w, axis=-1))
        decay = jnp.exp(b_last + m - m_new)
        w_s = jnp.exp(log_w - m_new[..., None])
        C_new = decay[..., None, None] * C + jnp.einsum("bhs,bhsv,bhsd->bhvd", w_s, vc, kc)
        n_new = decay[..., None] * nv + jnp.einsum("bhs,bhsd->bhd", w_s, kc)
        return (C_new, n_new, m_new), hout

    state, hs = lax.scan(step, state, xs)
    hs = jnp.moveaxis(hs.swapaxes(0, 1), 2, 3).reshape(b, n, h, dv)
    return hs.astype(v.dtype), state


def mlstm_mixer(pc, pl, gate_b, norm_g, ctx_out):
    def prep(p):
        q, k, v, o, gt = p
        b, n = q.shape[:2]
        q = q.reshape(b, n, MLSTM_HEADS, MLSTM_DK)
        k = k.reshape(b, n, MLSTM_HEADS, MLSTM_DK)
        v = v.reshape(b, n, MLSTM_HEADS, MLSTM_DV)
        gt = gt.astype(jnp.float32).reshape(b, n, 4, MLSTM_HEADS) + gate_b.astype(jnp.float32)
        fwd = (q, k, v, gt[:, :, 0], jax.nn.log_sigmoid(gt[:, :, 1]))
        bwd = tuple(jnp.flip(a, axis=1) for a in (q, k, v, gt[:, :, 2], jax.nn.log_sigmoid(gt[:, :, 3])))
        return fwd, bwd, o

    fc, bc, oc = prep(pc)
    fl, bl, ol = prep(pl)
    b = ol.shape[0]
    zero = (jnp.zeros((b, MLSTM_HEADS, MLSTM_DV, MLSTM_DK), jnp.float32),
            jnp.zeros((b, MLSTM_HEADS, MLSTM_DK), jnp.float32),
            jnp.zeros((b, MLSTM_HEADS), jnp.float32))
    hc_f, st_f = mlstm_chunked(*fc, zero)
    hl_f, _ = mlstm_chunked(*fl, st_f)
    hc_b, st_b = mlstm_chunked(*bc, zero)
    hl_b, _ = mlstm_chunked(*bl, st_b)
    g = norm_g.reshape(MLSTM_HEADS, MLSTM_DV)

    def finish(h, o):
        bb, n = h.shape[:2]
        h = rms_norm(h, g).reshape(bb, n, MLSTM_HEADS * MLSTM_DV)
        return h * jax.nn.sigmoid(o)

    yl = finish(hl_f + jnp.flip(hl_b, axis=1), ol)
    yc = finish(hc_f + jnp.flip(hc_b, axis=1), oc) if ctx_out else None
    return yc, yl


def diff_mixer(pc, pl, lam_params, norm_g, lam_init, rope, ctx_out):
    def heads(p):
        q, k, v = p
        b, n = q.shape[:2]
        return (q.reshape(b, n, 2 * DIFF_HEADS, DIFF_D),
                k.reshape(b, n, 2 * DIFF_HEADS, DIFF_D),
                v.reshape(b, n, DIFF_HEADS, 2 * DIFF_D))

    qc, kc, vc = heads(pc)
    ql, kl, vl = heads(pl)
    ql = apply_rope(ql, *rope)
    kl = apply_rope(kl, *rope)

    def pair(a):
        return a.reshape(a.shape[0], a.shape[1], DIFF_HEADS, 2, DIFF_D)

    lp = lam_params.astype(jnp.float32)
    lam = jnp.exp(jnp.sum(lp[0] * lp[1], axis=-1)) - jnp.exp(jnp.sum(lp[2] * lp[3], axis=-1)) + lam_init
    g = norm_g.reshape(DIFF_HEADS, 2 * DIFF_D)

    def finish(o):
        b, n = o.shape[:2]
        return (rms_norm(o, g) * (1.0 - lam_init)).reshape(b, n, DIFF_HEADS * 2 * DIFF_D)

    k_all = jnp.concatenate([pair(kc), pair(kl)], axis=1)
    v_all = jnp.concatenate([vc, vl], axis=1)
    yl = finish(diff_attention(pair(ql), k_all, v_all, lam))
    yc = finish(diff_attention(pair(qc), pair(kc), vc, lam)) if ctx_out else None
    return yc, yl


def swa_mixer(pc, pl, sink, rope, ctx_out):
    def heads(p):
        q, k, v = p
        b, n = q.shape[:2]
        return (q.reshape(b, n, SWA_HEADS, SWA_D),
                k.reshape(b, n, SWA_KV_HEADS, SWA_D),
                v.reshape(b, n, SWA_KV_HEADS, SWA_D))

    qc, kc, vc = heads(pc)
    ql, kl, vl = heads(pl)
    ql = apply_rope(ql, *rope)
    kl = apply_rope(kl, *rope)

    def flat(o):
        return o.reshape(o.shape[0], o.shape[1], SWA_HEADS * SWA_D)

    yl = flat(window_attention(ql, kl, vl, kc, vc, sink))
    yc = flat(dense_gqa(qc, kc, vc, sink)) if ctx_out else None
    return yc, yl


def gqa_mixer(pc, pl, q_norm_g, k_norm_g, rope, ctx_out):
    def heads(p):
        q, k, v = p
        b, n = q.shape[:2]
        q = rms_norm(q.reshape(b, n, GQA_HEADS, GQA_D), q_norm_g)
        k = rms_norm(k.reshape(b, n, GQA_KV_HEADS, GQA_D), k_norm_g)
        return q, k, v.reshape(b, n, GQA_KV_HEADS, GQA_D)

    qc, kc, vc = heads(pc)
    ql, kl, vl = heads(pl)
    ql = apply_rope(ql, *rope)
    kl = apply_rope(kl, *rope)

    def flat(o):
        return o.reshape(o.shape[0], o.shape[1], GQA_HEADS * GQA_D)

    yl = flat(dense_gqa(ql, jnp.concatenate([kc, kl], axis=1), jnp.concatenate([vc, vl], axis=1)))
    yc = flat(dense_gqa(qc, kc, vc)) if ctx_out else None
    return yc, yl


def merge_branches(ys, gate_pre, w_branch, w_out):
    b, n = gate_pre.shape[:2]
    gates = jax.nn.sigmoid(gate_pre.astype(jnp.float32)).astype(ys[0].dtype).reshape(b, n, N_BRANCHES, D_MODEL)
    acc = gates[:, :, 0] * (ys[0] @ w_branch[0])
    for i in range(1, N_BRANCHES):
        acc = acc + gates[:, :, i] * (ys[i] @ w_branch[i])
    return acc @ w_out


def hybrid_mixer(hc, hl, w_in, mlstm_gate_b, mlstm_norm_g, diff_lambda, diff_norm_g, lam_init,
                 swa_sink, gqa_q_norm_g, gqa_k_norm_g, w_branch, w_out, rope64, rope128, ctx_out):
    pc = split_cols(hc @ w_in)
    pl = split_cols(hl @ w_in)
    ya_c, ya_l = mlstm_mixer(pc[0:5], pl[0:5], mlstm_gate_b, mlstm_norm_g, ctx_out)
    yb_c, yb_l = diff_mixer(pc[5:8], pl[5:8], diff_lambda, diff_norm_g, lam_init, rope64, ctx_out)
    yc_c, yc_l = swa_mixer(pc[8:11], pl[8:11], swa_sink, rope64, ctx_out)
    yd_c, yd_l = gqa_mixer(pc[11:14], pl[11:14], gqa_q_norm_g, gqa_k_norm_g, rope128, ctx_out)
    out_l = merge_branches((ya_l, yb_l, yc_l, yd_l), pl[14], w_branch, w_out)
    out_c = merge_branches((ya_c, yb_c, yc_c, yd_c), pc[14], w_branch, w_out) if ctx_out else None
    return out_c, out_l


def swiglu(h, w_up, w_down):
    gate, up = jnp.split(h @ w_up, 2, axis=-1)
    return (jax.nn.silu(gate) * up) @ w_down


def setup_inputs(seed: int = 0) -> dict:
    key = jax.random.key(seed)
    ks = jax.random.split(key, 24)
    f32 = jnp.float32

    def nrm(k, shape, s):
        return s * jax.random.normal(k, shape, f32)

    zeros_h = jnp.zeros((MLSTM_HEADS,), f32)
    forget_base = jnp.linspace(3.0, 6.0, MLSTM_HEADS, dtype=f32)
    gate_base = jnp.stack([zeros_h, forget_base, zeros_h, forget_base])
    return {
        "x": nrm(ks[0], (BATCH, SEQ, D_MODEL), 1.0),
        "c": nrm(ks[1], (BATCH, D_MODEL), 1.0),
        "ctx": nrm(ks[2], (BATCH, CTX_LEN, D_MODEL), 1.0),
        "c_ctx": nrm(ks[3], (D_MODEL,), 1.0),
        "ada_w": nrm(ks[4], (DEPTH, D_MODEL, 6 * D_MODEL), 0.5 * D_MODEL ** -0.5),
        "ada_b": nrm(ks[5], (DEPTH, 6 * D_MODEL), 0.01),
        "norm1_g": 1.0 + nrm(ks[6], (DEPTH, D_MODEL), 0.01),
        "w_in": nrm(ks[7], (DEPTH, D_MODEL, D_IN), D_MODEL ** -0.5),
        "mlstm_gate_b": gate_base + nrm(ks[8], (DEPTH, 4, MLSTM_HEADS), 0.1),
        "mlstm_norm_g": 1.0 + nrm(ks[9], (DEPTH, MLSTM_HEADS * MLSTM_DV), 0.01),
        "diff_lambda": nrm(ks[10], (DEPTH, 4, DIFF_HEADS, DIFF_D), 0.1),
        "diff_norm_g": 1.0 + nrm(ks[11], (DEPTH, DIFF_HEADS * 2 * DIFF_D), 0.01),
        "swa_sink": nrm(ks[12], (DEPTH, SWA_HEADS), 0.5),
        "gqa_q_norm_g": 1.0 + nrm(ks[13], (DEPTH, GQA_D), 0.01),
        "gqa_k_norm_g": 1.0 + nrm(ks[14], (DEPTH, GQA_D), 0.01),
        "w_branch": nrm(ks[15], (DEPTH, N_BRANCHES, BRANCH_WIDTH, D_MODEL), BRANCH_WIDTH ** -0.5),
        "w_out": nrm(ks[16], (DEPTH, D_MODEL, D_MODEL), D_MODEL ** -0.5),
        "norm2_g": 1.0 + nrm(ks[17], (DEPTH, D_MODEL), 0.01),
        "w_up": nrm(ks[18], (DEPTH, D_MODEL, 2 * FFN_HIDDEN), D_MODEL ** -0.5),
        "w_down": nrm(ks[19], (DEPTH, FFN_HIDDEN, D_MODEL), FFN_HIDDEN ** -0.5),
        "final_norm_g": 1.0 + nrm(ks[20], (D_MODEL,), 0.01),
    }


def reference(x, c, ctx, c_ctx, ada_w, ada_b, norm1_g, w_in, mlstm_gate_b, mlstm_norm_g,
              diff_lambda, diff_norm_g, swa_sink, gqa_q_norm_g, gqa_k_norm_g, w_branch, w_out,
              norm2_g, w_up, w_down, final_norm_g):
    n_lat = x.shape[1]
    rows = n_lat // GRID_W
    rope64 = axial_rope_tables(rows, DIFF_D)
    rope128 = axial_rope_tables(rows, GQA_D)
    s_lat = jax.nn.silu(c.astype(jnp.float32))
    s_ctx = jax.nn.silu(c_ctx.astype(jnp.float32))
    xc = ctx
    for l in range(DEPTH):
        last = l == DEPTH - 1
        mod_l = (s_lat @ ada_w[l] + ada_b[l]).astype(x.dtype)[:, None, :]
        mod_c = (s_ctx @ ada_w[l] + ada_b[l]).astype(x.dtype)[None, None, :]
        sh1_l, sc1_l, g1_l, sh2_l, sc2_l, g2_l = jnp.split(mod_l, 6, axis=-1)
        sh1_c, sc1_c, g1_c, sh2_c, sc2_c, g2_c = jnp.split(mod_c, 6, axis=-1)
        lam_init = 0.8 - 0.6 * math.exp(-0.3 * l)

        hl = modulate(rms_norm(x, norm1_g[l]), sh1_l, sc1_l)
        hc = modulate(rms_norm(xc, norm1_g[l]), sh1_c, sc1_c)
        out_c, out_l = hybrid_mixer(hc, hl, w_in[l], mlstm_gate_b[l], mlstm_norm_g[l], diff_lambda[l],
                                    diff_norm_g[l], lam_init, swa_sink[l], gqa_q_norm_g[l], gqa_k_norm_g[l],
                                    w_branch[l], w_out[l], rope64, rope128, not last)
        x = x + g1_l * out_l
        hl = modulate(rms_norm(x, norm2_g[l]), sh2_l, sc2_l)
        x = x + g2_l * swiglu(hl, w_up[l], w_down[l])
        if not last:
            xc = xc + g1_c * out_c
            hc = modulate(rms_norm(xc, norm2_g[l]), sh2_c, sc2_c)
            xc = xc + g2_c * swiglu(hc, w_up[l], w_down[l])
    return rms_norm(x, final_norm_g)
```

```python
import math
from contextlib import ExitStack

import numpy as np
import concourse.bass as bass
import concourse.mybir as mybir
from concourse.bass_utils import run_bass_kernel_spmd

F32 = mybir.dt.float32
BF16 = mybir.dt.bfloat16
AF = mybir.ActivationFunctionType
ALU = mybir.AluOpType

D = 2048
KC = 16
NCTX = 256
GRID_W = 64
EPS = 1e-6
FFN = 5632
DEPTH = 2


def _merge(d, t):
    for k, (s, v) in t.items():
        if k not in d or d[k][1] < v:
            d[k] = (s, v)


class Buf:
    __slots__ = ("w", "r", "name")

    def __init__(self, name=""):
        self.w = {}
        self.r = {}
        self.name = name


class EngQ:
    ROT = 30000

    def __init__(self, K, name, eng, skip_self=False):
        self.K = K
        self.name = name
        self.eng = eng
        self.seen = {}
        self.skip_self = skip_self
        self.own = set()
        self._new_sem()

    def _new_sem(self):
        self.sem = self.K.nc.alloc_semaphore(f"q{self.name}{self.K.nsem}")
        self.K.nsem += 1
        self.cnt = 0
        self.own.add(id(self.sem))

    def wait(self, deps):
        for key, (sem, val) in deps.items():
            if self.skip_self and key in self.own:
                continue
            if self.seen.get(key, 0) < val:
                self.eng.wait_ge(sem, val)
                self.seen[key] = val

    def emit(self, ins):
        if self.cnt >= self.ROT:
            self._new_sem()
        self.cnt += 1
        ins.then_inc(self.sem, 1)
        return {id(self.sem): (self.sem, self.cnt)}


class DmaQ:
    NS = 16

    def __init__(self, K, name, eng):
        self.K = K
        self.name = name
        self.eng = eng
        self.seen = {}
        self.sems = [K.nc.alloc_semaphore(f"d{name}{i}") for i in range(self.NS)]
        K.nsem += self.NS
        self.cnts = [0] * self.NS
        self.i = 0

    def wait(self, deps):
        for key, (sem, val) in deps.items():
            if self.seen.get(key, 0) < val:
                self.eng.wait_ge(sem, val)
                self.seen[key] = val

    def pre(self):
        j = self.i % self.NS
        if self.cnts[j] > 0:
            key = id(self.sems[j])
            val = 16 * self.cnts[j]
            if self.seen.get(key, 0) < val:
                self.eng.wait_ge(self.sems[j], val)
                self.seen[key] = val

    def emit(self, ins):
        j = self.i % self.NS
        self.i += 1
        if self.cnts[j] >= 30000:
            self.sems[j] = self.K.nc.alloc_semaphore(f"d{self.name}{self.K.nsem}")
            self.K.nsem += 1
            self.cnts[j] = 0
        self.cnts[j] += 1
        ins.then_inc(self.sems[j], 16)
        return {id(self.sems[j]): (self.sems[j], 16 * self.cnts[j])}


class RPool:
    def __init__(self, K, es, name, n, shape, dtype, psum=False):
        self.tiles = []
        for i in range(n):
            if psum:
                t = es.enter_context(K.nc.psum_tensor(f"p_{name}{i}", shape, dtype))
            else:
                t = es.enter_context(K.nc.sbuf_tensor(f"s_{name}{i}", shape, dtype))
            self.tiles.append((t, Buf(f"{name}{i}")))
        self.i = 0

    def get(self):
        t = self.tiles[self.i % len(self.tiles)]
        self.i += 1
        return t


class Ker:
    def __init__(self, nc):
        self.nc = nc
        self.nsem = 0
        self.pe = EngQ(self, "pe", nc.tensor, skip_self=True)
        self.act = EngQ(self, "act", nc.scalar)
        self.dve = EngQ(self, "dve", nc.vector)
        self.pool = EngQ(self, "pool", nc.gpsimd)
        self.ld = DmaQ(self, "ld", nc.sync)
        self.st = DmaQ(self, "st", nc.gpsimd)
        self.st.seen = self.pool.seen
        self.final = {}

    def op(self, q, fn, reads=(), writes=()):
        deps = {}
        for b in reads:
            _merge(deps, b.w)
        for b in writes:
            _merge(deps, b.w)
            _merge(deps, b.r)
        q.wait(deps)
        if isinstance(q, DmaQ):
            q.pre()
        tok = q.emit(fn())
        for b in reads:
            _merge(b.r, tok)
        for b in writes:
            b.w = dict(tok)
            b.r = {}
        return tok

    def dma_in(self, out, in_, reads=(), writes=()):
        return self.op(self.ld, lambda: self.nc.sync.dma_start(out=out, in_=in_), reads, writes)

    def dma_out(self, out, in_, reads=(), writes=()):
        tok = self.op(self.st, lambda: self.nc.gpsimd.dma_start(out=out, in_=in_), reads, writes)
        _merge(self.final, tok)
        return tok

    def mm(self, out, pairs, reads, wbuf, start=True, stop=True):
        deps = {}
        for b in reads:
            _merge(deps, b.w)
        _merge(deps, wbuf.w)
        _merge(deps, wbuf.r)
        self.pe.wait(deps)
        n = len(pairs)
        ins = None
        for i, (l, r) in enumerate(pairs):
            ins = self.nc.tensor.matmul(out, l, r, start=(start and i == 0), stop=(stop and i == n - 1))
        tok = self.pe.emit(ins)
        for b in reads:
            _merge(b.r, tok)
        wbuf.w = dict(tok)
        wbuf.r = {}
        return tok

    def barrier(self):
        toks = {}
        for q in (self.pe, self.act, self.dve, self.pool):
            if q.cnt > 0:
                toks[id(q.sem)] = (q.sem, q.cnt)
        for dq in (self.ld, self.st):
            for j in range(dq.NS):
                if dq.cnts[j] > 0:
                    toks[id(dq.sems[j])] = (dq.sems[j], 16 * dq.cnts[j])
        for q in (self.pe, self.act, self.dve, self.pool, self.ld):
            q.wait(toks)

    def finish(self):
        self.pool.wait(self.final)
        allq = {}
        for q in (self.pe, self.act, self.dve, self.pool):
            if q.cnt > 0:
                allq[id(q.sem)] = (q.sem, q.cnt)
        self.pool.wait(allq)


class Cfg:
    def __init__(self, N=16384, last=False, lam_init=0.2, debug=()):
        self.N = N
        self.NO = N // 4
        self.NOT = N - self.NO
        self.NA = NCTX + N + 256
        self.OWN0 = NCTX
        self.OTH0 = NCTX + self.NO
        self.HALO0 = NCTX + N
        self.NOC = NCTX + self.NO
        self.NKV = NCTX + N
        self.last = last
        self.lam_init = lam_init
        self.debug = debug
        self.TT = 512
        t = [(0, NCTX, "ctx")]
        for i in range(self.NO // 512):
            t.append((self.OWN0 + 512 * i, 512, "own"))
        for i in range(self.NOT // 512):
            t.append((self.OTH0 + 512 * i, 512, "oth"))
        t.append((self.HALO0, 256, "halo"))
        self.tiles = t
        self.qtiles = [x for x in t if x[2] in ("ctx", "own")]


IN_SPLITS = (256, 256, 512, 512, 16, 512, 512, 512, 512, 128, 128, 512, 256, 256, 8192)
IN_OFF = [0]
for _s in IN_SPLITS:
    IN_OFF.append(IN_OFF[-1] + _s)
D_IN = IN_OFF[-1]


def _seg(gi, lo=0, hi=None):
    w = IN_SPLITS[gi] if hi is None else hi
    return (IN_OFF[gi] + lo, IN_OFF[gi] + w)


def _blocks_from_segs(segs):
    blocks = []
    cur = []
    off = 0
    for (c0, c1) in segs:
        while c0 < c1:
            take = min(c1 - c0, 512 - off)
            cur.append((c0, c0 + take, off))
            off += take
            c0 += take
            if off == 512:
                blocks.append(cur)
                cur = []
                off = 0
    if cur:
        blocks.append(cur)
    return blocks


WA_FM_SEGS = [_seg(6), _seg(12), _seg(9)]
WA_TM_SEGS = [_seg(1), _seg(13), _seg(2), _seg(7), _seg(10), _seg(4)]
WB_SEGS = [_seg(0), _seg(1), _seg(3), _seg(5), _seg(8), _seg(11)]
WG_SEGS = [_seg(14)]


class Prog:
    def __init__(self, cfg):
        self.cfg = cfg
        self.nc = bass.Bass("TRN2", target_bir_lowering=False)
        self.K = Ker(self.nc)
        self.dram = {}
        self.dbuf = {}

    def din(self, name, shape, dt=F32):
        self.dram[name] = self.nc.dram_tensor(name, list(shape), dt, kind="ExternalInput").ap()
        self.dbuf[name] = Buf(name)
        return self.dram[name]

    def dout(self, name, shape, dt=F32):
        self.dram[name] = self.nc.dram_tensor(name, list(shape), dt, kind="ExternalOutput").ap()
        self.dbuf[name] = Buf(name)
        return self.dram[name]

    def dscr(self, name, shape, dt=BF16):
        if name in self.cfg.debug:
            return self.dout(name, shape, dt)
        self.dram[name] = self.nc.dram_tensor(name, list(shape), dt).ap()
        self.dbuf[name] = Buf(name)
        return self.dram[name]

    def dump(self, name, ap, shape, buf, dt=F32):
        d = self.dout(name, shape, dt)
        self.K.dma_out(d, ap, reads=[buf], writes=[self.dbuf[name]])

    def sb(self, es, name, shape, dt):
        return es.enter_context(self.nc.sbuf_tensor("s_" + name, list(shape), dt))

    def declare(self):
        c = self.cfg
        self.din("xT", [D, c.NA])
        self.din("cvec", [128, KC, 2])
        self.din("ada_w", [D, 6 * D])
        self.din("ada_b", [128, 96, 2])
        self.din("n1g", [128, KC, 2])
        self.din("n2g", [128, KC, 2])
        self.din("fng", [128, KC])
        self.din("w_in", [D, D_IN])
        self.din("gate_b", [128, 16])
        self.din("mng", [128, 4])
        self.din("dlam", [128, 4, 4, 64])
        self.din("dng", [128, 4])
        self.din("sink", [128, 8])
        self.din("gqg", [128, 1])
        self.din("gkg", [128, 1])
        self.din("w_branch", [4, 512, D])
        self.din("w_out", [D, D])
        self.din("w_up", [D, 2 * FFN])
        self.din("w_down", [FFN, D])
        self.din("cos64", [128, c.NA])
        self.din("sin64", [128, c.NA])
        self.din("cos128", [128, c.NA])
        self.din("sin128", [128, c.NA])
        self.din("rmats", [128, 2, 128])
        self.din("ident", [128, 128])
        self.din("trimask", [128, 2, 128])
        self.din("swamask", [128, 8, 512])
        self.din("mflags", [128, 2, c.NOT // 128])
        self.dout("outT", [D, c.NO])
        if not c.last:
            self.dout("ctxoT", [D, NCTX])
        nblk = lambda segs: len(_blocks_from_segs(segs))
        self.dscr("wA_fm", [nblk(WA_FM_SEGS), 1, 128, KC * 512])
        self.dscr("wA_tm", [nblk(WA_TM_SEGS), 1, 128, KC * 512])
        self.dscr("wB", [nblk(WB_SEGS), 1, 128, KC * 512])
        self.dscr("wG", [16, 1, 128, KC * 512])
        self.dscr("wBr", [16, 1, 128, 4 * 512])
        self.dscr("wO", [4, 1, 128, KC * 512])
        self.dscr("wU", [22, 1, 128, KC * 512])
        self.dscr("wD", [4, 4, 128, 11 * 512])
        self.dscr("hT", [D, c.NOC])
        self.dscr("dkT", [512, c.NA])
        self.dscr("gkT", [256, c.NA])
        self.dscr("skT", [128, c.NA])
        self.dscr("kvTM", [c.NA, 1664])
        self.dscr("gatesTM", [c.NA, 16], F32)
        self.dscr("mqT", [256, c.NOC])
        self.dscr("mkT", [256, c.NOC])
        self.dscr("soT", [512, c.NOC])
        self.dscr("dqT", [512, c.NOC])
        self.dscr("sqT", [512, c.NOC])
        self.dscr("gqT", [512, c.NOC])
        self.dscr("yT", [4, 512, c.NOC])

    def consts(self, es):
        K, nc = self.K, self.nc
        self.cb = Buf("consts")
        self.ones_bf = self.sb(es, "ones_bf", [128, 128], BF16)
        K.op(K.dve, lambda: nc.vector.memset(self.ones_bf[:], 1.0), writes=[self.cb])
        self.eps_t = self.sb(es, "eps_t", [128, 1], F32)
        K.op(K.dve, lambda: nc.vector.memset(self.eps_t[:], EPS), writes=[self.cb])
        self.ones_f = self.sb(es, "ones_f", [128, 128], F32)
        K.op(K.dve, lambda: nc.vector.memset(self.ones_f[:], 1.0), writes=[self.cb])
        tmp = self.sb(es, "c_tmp", [128, 2, 128], F32)
        K.dma_in(tmp[:], self.dram["rmats"], writes=[self.cb])
        self.rm = self.sb(es, "rm_bf", [128, 2, 128], BF16)
        K.op(K.dve, lambda: nc.vector.tensor_copy(self.rm[:], tmp[:]), reads=[self.cb], writes=[self.cb])
        self.ident_f = self.sb(es, "ident_f", [128, 128], F32)
        K.dma_in(self.ident_f[:], self.dram["ident"], writes=[self.cb])
        self.ident_bf = self.sb(es, "ident_bf", [128, 128], BF16)
        K.op(K.dve, lambda: nc.vector.tensor_copy(self.ident_bf[:], self.ident_f[:]), reads=[self.cb], writes=[self.cb])
        self.gqg = self.sb(es, "gqg", [128, 1], F32)
        K.dma_in(self.gqg[:], self.dram["gqg"], writes=[self.cb])
        self.gkg = self.sb(es, "gkg", [128, 1], F32)
        K.dma_in(self.gkg[:], self.dram["gkg"], writes=[self.cb])
        self.A1 = self.sb(es, "A1", [128, KC, 2], F32)
        self.B1 = self.sb(es, "B1", [128, KC, 2], F32)
        self.G1 = self.sb(es, "G1", [128, KC, 2], F32)
        self.A2 = self.sb(es, "A2", [128, KC, 2], F32)
        self.B2 = self.sb(es, "B2", [128, KC, 2], F32)
        self.G2 = self.sb(es, "G2", [128, KC, 2], F32)

    def cast_weights(self):
        K, nc, c = self.K, self.nc, self.cfg
        self.K.barrier()
        with ExitStack() as es:
            stf = RPool(K, es, "cst_f", 2, [128, KC * 512], F32)
            stb = RPool(K, es, "cst_b", 2, [128, KC * 512], BF16)
            cnt = [0]

            def do_block(dst_name, blk, kg, srcs, kpg):
                f, fb = stf.get()
                b, bb = stb.get()
                f3 = f[:, 0:kpg * 512].rearrange("p (k c) -> p k c", c=512)
                b3 = b[:, 0:kpg * 512].rearrange("p (k c) -> p k c", c=512)
                wtot = 0
                for (src, off, w) in srcs:
                    K.dma_in(f3[:, :, off:off + w], src, writes=[fb])
                    wtot = max(wtot, off + w)
                eng = [K.dve, K.pool, K.act][cnt[0] % 3]
                cnt[0] += 1
                if eng is K.act:
                    K.op(eng, lambda: nc.scalar.copy(b3[:, :, 0:wtot], f3[:, :, 0:wtot]), reads=[fb], writes=[bb])
                elif eng is K.dve:
                    K.op(eng, lambda: nc.vector.tensor_copy(b3[:, :, 0:wtot], f3[:, :, 0:wtot]), reads=[fb], writes=[bb])
                else:
                    K.op(eng, lambda: nc.gpsimd.tensor_copy(b3[:, :, 0:wtot], f3[:, :, 0:wtot]), reads=[fb], writes=[bb])
                K.dma_out(self.dram[dst_name][blk, kg, :, 0:kpg * 512], b[:, 0:kpg * 512], reads=[bb],
                          writes=[self.dbuf[dst_name]])

            win3 = self.dram["w_in"].rearrange("(k p) c -> p k c", p=128)
            for name, segs in (("wA_fm", WA_FM_SEGS), ("wA_tm", WA_TM_SEGS), ("wB", WB_SEGS), ("wG", WG_SEGS)):
                for blk, pieces in enumerate(_blocks_from_segs(segs)):
                    do_block(name, blk, 0, [(win3[:, :, c0:c1], off, c1 - c0) for (c0, c1, off) in pieces], KC)
            for br in range(4):
                wb3 = self.dram["w_branch"][br].rearrange("(k p) c -> p k c", p=128)
                for j in range(4):
                    do_block("wBr", br * 4 + j, 0, [(wb3[:, :, j * 512:(j + 1) * 512], 0, 512)], 4)
            wo3 = self.dram["w_out"].rearrange("(k p) c -> p k c", p=128)
            for j in range(4):
                do_block("wO", j, 0, [(wo3[:, :, j * 512:(j + 1) * 512], 0, 512)], KC)
            wu3 = self.dram["w_up"].rearrange("(k p) c -> p k c", p=128)
            for j in range(22):
                do_block("wU", j, 0, [(wu3[:, :, j * 512:(j + 1) * 512], 0, 512)], KC)
            wd3 = self.dram["w_down"].rearrange("(k p) c -> p k c", p=128)
            for j in range(4):
                for kg in range(4):
                    do_block("wD", j, kg, [(wd3[:, kg * 11:(kg + 1) * 11, j * 512:(j + 1) * 512], 0, 512)], 11)

    def phase_mod(self):
        K, nc = self.K, self.nc
        self.K.barrier()
        with ExitStack() as es:
            cv = self.sb(es, "m_cv", [128, KC, 2], F32)
            sv = self.sb(es, "m_sv", [128, KC, 2], F32)
            ab = self.sb(es, "m_ab", [128, 96, 2], F32)
            mod = self.sb(es, "m_mod", [128, 96, 2], F32)
            g1 = self.sb(es, "m_g1", [128, KC, 2], F32)
            g2 = self.sb(es, "m_g2", [128, KC, 2], F32)
            b = Buf("modsmall")
            K.dma_in(cv[:], self.dram["cvec"], writes=[b])
            K.dma_in(ab[:], self.dram["ada_b"], writes=[b])
            K.dma_in(g1[:], self.dram["n1g"], writes=[b])
            K.dma_in(g2[:], self.dram["n2g"], writes=[b])
            K.op(K.act, lambda: nc.scalar.activation(out=sv[:], in_=cv[:], func=AF.Silu), reads=[b], writes=[b])
            wst = RPool(K, es, "m_w", 2, [128, KC, 512], F32)
            self.psum = RPool(K, es, "psm", 2, [128, 512], F32, psum=True)
            ps, pb = self.psum.get()
            aw3 = self.dram["ada_w"].rearrange("(k p) c -> p k c", p=128)
            for blk in range(24):
                w, wb = wst.get()
                K.dma_in(w[:], aw3[:, :, blk * 512:(blk + 1) * 512], writes=[wb])
                for j in range(4):
                    fc = blk * 4 + j
                    K.mm(ps[:, fc * 2:fc * 2 + 2],
                         [(w[:, kc, j * 128:(j + 1) * 128], sv[:, kc, :]) for kc in range(KC)],
                         reads=[wb, b], wbuf=pb)
            mb = Buf("mod")
            K.op(K.dve, lambda: nc.vector.tensor_tensor(out=mod[:].rearrange("p a b -> p (a b)"), in0=ps[:, 0:192],
                                                        in1=ab[:].rearrange("p a b -> p (a b)"), op=ALU.add),
                 reads=[pb, b], writes=[mb])
            cbuf = self.cb

            def mk_a(dst, gain, sc0):
                K.op(K.dve, lambda: nc.vector.scalar_tensor_tensor(out=dst[:], in0=mod[:, sc0:sc0 + 16, :], scalar=1.0,
                                                                   in1=gain[:], op0=ALU.add, op1=ALU.mult),
                     reads=[mb, b], writes=[cbuf])

            def cp(dst, s0):
                K.op(K.dve, lambda: nc.vector.tensor_copy(dst[:], mod[:, s0:s0 + 16, :]), reads=[mb], writes=[cbuf])

            cp(self.B1, 0)
            mk_a(self.A1, g1, 16)
            cp(self.G1, 32)
            cp(self.B2, 48)
            mk_a(self.A2, g2, 64)
            cp(self.G2, 80)

    def ep_store(self, src_ap, src_buf, dname, r0, c0, TT):
        self.K.dma_out(self.dram[dname][r0:r0 + 128, c0:c0 + TT], src_ap, reads=[src_buf], writes=[self.dbuf[dname]])

    def ep_copy(self, ps, pb, TT, dname, r0, c0, scale=1.0, func=None):
        K, nc = self.K, self.nc
        y, yb = self.ybp.get()
        f = AF.Copy if func is None else func
        K.op(K.act, lambda: nc.scalar.activation(out=y[:, :TT], in_=ps[:, :TT], func=f, scale=scale), reads=[pb], writes=[yb])
        self.ep_store(y[:, :TT], yb, dname, r0, c0, TT)

    def rope_from(self, y, yb, TT, rt, rtb, which, dname, r0, c0):
        K, nc = self.K, self.nc
        ci = 0 if which == 64 else 2
        ridx = 0 if which == 64 else 1
        rp, rpb = self.psum.get()
        K.mm(rp[:, :TT], [(self.rm[:, ridx, :], y[:, :TT])], reads=[yb, self.cb], wbuf=rpb)
        t1, t1b = self.tfp.get()
        K.op(K.dve, lambda: nc.vector.tensor_tensor(out=t1[:, :TT], in0=y[:, :TT], in1=rt[:, ci, :TT], op=ALU.mult),
             reads=[yb, rtb], writes=[t1b])
        t2, t2b = self.tfp.get()
        K.op(K.dve, lambda: nc.vector.tensor_tensor(out=t2[:, :TT], in0=rp[:, :TT], in1=rt[:, ci + 1, :TT], op=ALU.mult),
             reads=[rpb, rtb], writes=[t2b])
        o, ob = self.ybp.get()
        K.op(K.pool, lambda: nc.gpsimd.tensor_tensor(out=o[:, :TT], in0=t1[:, :TT], in1=t2[:, :TT], op=ALU.add),
             reads=[t1b, t2b], writes=[ob])
        self.ep_store(o[:, :TT], ob, dname, r0, c0, TT)

    def ep_rope(self, ps, pb, TT, rt, rtb, which, dname, r0, c0):
        K, nc = self.K, self.nc
        y, yb = self.ybp.get()
        K.op(K.act, lambda: nc.scalar.copy(y[:, :TT], ps[:, :TT]), reads=[pb], writes=[yb])
        self.rope_from(y, yb, TT, rt, rtb, which, dname, r0, c0)

    def rstd_from_ps(self, ssp, sspb, TT, n):
        K, nc = self.K, self.nc
        r1, r1b = self.rsp.get()
        K.op(K.act, lambda: nc.scalar.activation(out=r1[:, :TT], in_=ssp[:, :TT], func=AF.Sqrt, bias=self.eps_t[:, 0:1],
                                                 scale=1.0 / n), reads=[sspb, self.cb], writes=[r1b])
        K.op(K.dve, lambda: nc.vector.reciprocal(out=r1[:, :TT], in_=r1[:, :TT]), reads=[r1b], writes=[r1b])
        return r1, r1b

    def ep_norm_rope(self, ps, pb, TT, gain, rt, rtb, dname, r0, c0):
        K, nc = self.K, self.nc
        sq, sqb = self.ybp.get()
        K.op(K.act, lambda: nc.scalar.activation(out=sq[:, :TT], in_=ps[:, :TT], func=AF.Square), reads=[pb], writes=[sqb])
        ssp, sspb = self.psum.get()
        K.mm(ssp[:, :TT], [(self.ones_bf[:], sq[:, :TT])], reads=[sqb, self.cb], wbuf=sspb)
        r1, r1b = self.rstd_from_ps(ssp, sspb, TT, 128.0)
        y, yb = self.ybp.get()
        K.op(K.dve, lambda: nc.vector.scalar_tensor_tensor(out=y[:, :TT], in0=ps[:, :TT], scalar=gain[:, 0:1],
                                                           in1=r1[:, :TT], op0=ALU.mult, op1=ALU.mult),
             reads=[pb, r1b, self.cb], writes=[yb])
        self.rope_from(y, yb, TT, rt, rtb, 128, dname, r0, c0)

    def norm_mod(self, es_tile, xt, xtb, sqt, sqtb, hT, hTb, TT, A, Bv, v):
        K, nc = self.K, self.nc
        K.op(K.act, lambda: nc.scalar.activation(out=sqt[:, :, :TT], in_=xt[:, :, :TT], func=AF.Square),
             reads=[xtb], writes=[sqtb])
        ssp, sspb = self.psum.get()
        K.mm(ssp[:, :TT], [(self.ones_bf[:], sqt[:, kc, :TT]) for kc in range(KC)], reads=[sqtb, self.cb], wbuf=sspb)
        r1, r1b = self.rstd_from_ps(ssp, sspb, TT, float(D))
        for kc in range(KC):
            t, tb = self.tfp.get()
            K.op(K.dve, lambda: nc.vector.scalar_tensor_tensor(out=t[:, :TT], in0=xt[:, kc, :TT], scalar=A[:, kc, v:v + 1],
                                                               in1=r1[:, :TT], op0=ALU.mult, op1=ALU.mult),
                 reads=[xtb, r1b, self.cb], writes=[tb])
            K.op(K.act, lambda: nc.scalar.activation(out=hT[:, kc, :TT], in_=t[:, :TT], func=AF.Identity,
                                                     bias=Bv[:, kc, v:v + 1], scale=1.0),
                 reads=[tb, self.cb], writes=[hTb])

    def phase1a(self):
        K, nc, c = self.K, self.nc, self.cfg
        self.K.barrier()
        with ExitStack() as es:
            self.psum = RPool(K, es, "psa", 8, [128, 512], F32, psum=True)
            self.ybp = RPool(K, es, "a_yb", 6, [128, 512], BF16)
            self.tfp = RPool(K, es, "a_tf", 4, [128, 512], F32)
            self.rsp = RPool(K, es, "a_rs", 2, [128, 512], F32)
            xt = self.sb(es, "a_xt", [128, KC, 512], F32); xtb = Buf()
            sqt = self.sb(es, "a_sq", [128, KC, 512], BF16); sqtb = Buf()
            hT = self.sb(es, "a_hT", [128, KC, 512], BF16); hTb = Buf()
            rt = self.sb(es, "a_rt", [128, 4, 512], F32); rtb = Buf()
            wfm0 = self.sb(es, "a_wfm0", [128, KC, 512], BF16)
            wfm1 = self.sb(es, "a_wfm1", [128, KC, 384], BF16)
            wtm = [self.sb(es, f"a_wtm{i}", [128, KC, 512], BF16) for i in range(3)]
            wtm3 = self.sb(es, "a_wtm3", [128, KC, 144], BF16)
            gb = self.sb(es, "a_gb", [128, 16], F32)
            wb = Buf("wA")
            r3 = lambda name, blk: self.dram[name][blk, 0].rearrange("p (k c) -> p k c", c=512)
            K.dma_in(wfm0[:], r3("wA_fm", 0), reads=[self.dbuf["wA_fm"]], writes=[wb])
            K.dma_in(wfm1[:], r3("wA_fm", 1)[:, :, 0:384], reads=[self.dbuf["wA_fm"]], writes=[wb])
            for i in range(3):
                K.dma_in(wtm[i][:], r3("wA_tm", i), reads=[self.dbuf["wA_tm"]], writes=[wb])
            K.dma_in(wtm3[:], r3("wA_tm", 3)[:, :, 0:144], reads=[self.dbuf["wA_tm"]], writes=[wb])
            K.dma_in(gb[:], self.dram["gate_b"], writes=[wb])
            stg = RPool(K, es, "a_stg", 2, [128, 1664], BF16)
            gst = RPool(K, es, "a_gst", 2, [128, 16], F32)
            x3 = self.dram["xT"].rearrange("(k p) a -> p k a", p=128)
            h3 = self.dram["hT"].rearrange("(k p) a -> p k a", p=128)
            for (a0, TT, kind) in c.tiles:
                v = 1 if kind == "ctx" else 0
                K.dma_in(xt[:, :, :TT], x3[:, :, a0:a0 + TT], writes=[xtb])
                for i, nm in enumerate(("cos64", "sin64", "cos128", "sin128")):
                    K.dma_in(rt[:, i, :TT], self.dram[nm][:, a0:a0 + TT], writes=[rtb])
                self.norm_mod(es, xt, xtb, sqt, sqtb, hT, hTb, TT, self.A1, self.B1, v)
                if kind in ("ctx", "own"):
                    K.dma_out(h3[:, :, a0:a0 + TT], hT[:, :, :TT], reads=[hTb], writes=[self.dbuf["hT"]])
                for ch in range(7):
                    wt = wfm0 if ch < 4 else wfm1
                    j = ch if ch < 4 else ch - 4
                    ps, pb = self.psum.get()
                    K.mm(ps[:, :TT], [(wt[:, kc, j * 128:(j + 1) * 128], hT[:, kc, :TT]) for kc in range(KC)],
                         reads=[wb, hTb], wbuf=pb)
                    if ch < 4:
                        self.ep_rope(ps, pb, TT, rt, rtb, 64, "dkT", ch * 128, a0)
                    elif ch < 6:
                        self.ep_norm_rope(ps, pb, TT, self.gkg, rt, rtb, "gkT", (ch - 4) * 128, a0)
                    else:
                        self.ep_rope(ps, pb, TT, rt, rtb, 64, "skT", 0, a0)
                for sub in range(TT // 128):
                    s, sbf = stg.get()
                    g, gbf = gst.get()
                    pss = []
                    for gi in range(4):
                        w = 512 if gi < 3 else 144
                        wt = wtm[gi] if gi < 3 else wtm3
                        ps, pb = self.psum.get()
                        K.mm(ps[:, :w], [(hT[:, kc, sub * 128:(sub + 1) * 128], wt[:, kc, 0:w]) for kc in range(KC)],
                             reads=[wb, hTb], wbuf=pb)
                        pss.append((ps, pb))
                    (p0, b0), (p1, b1), (p2, b2), (p3, b3) = pss
                    K.op(K.act, lambda: nc.scalar.activation(out=s[:, 0:256], in_=p0[:, 0:256], func=AF.Copy, scale=0.125),
                         reads=[b0], writes=[sbf])
                    K.op(K.dve, lambda: nc.vector.tensor_copy(s[:, 256:512], p0[:, 256:512]), reads=[b0], writes=[sbf])
                    K.op(K.dve, lambda: nc.vector.tensor_copy(s[:, 512:1024], p1[:, 0:512]), reads=[b1], writes=[sbf])
                    K.op(K.act, lambda: nc.scalar.copy(s[:, 1024:1536], p2[:, 0:512]), reads=[b2], writes=[sbf])
                    K.op(K.dve, lambda: nc.vector.tensor_copy(s[:, 1536:1664], p3[:, 0:128]), reads=[b3], writes=[sbf])
                    K.op(K.dve, lambda: nc.vector.tensor_tensor(out=g[:], in0=p3[:, 128:144], in1=gb[:], op=ALU.add),
                         reads=[b3, wb], writes=[gbf])
                    r0 = a0 + sub * 128
                    K.dma_out(self.dram["kvTM"][r0:r0 + 128, :], s[:], reads=[sbf], writes=[self.dbuf["kvTM"]])
                    K.dma_out(self.dram["gatesTM"][r0:r0 + 128, :], g[:], reads=[gbf], writes=[self.dbuf["gatesTM"]])

    def fm_gemm(self, act, act_bufs, TT, wname, blocks, nkg, kpg, cb, wpool, nch=None):
        K = self.K
        for blk in blocks:
            n = 4 if nch is None else nch(blk)
            pss = [self.psum.get() for _ in range(n)]
            for kg in range(nkg):
                wt, wtb = wpool.get()
                K.dma_in(wt[:, 0:kpg * 512], self.dram[wname][blk, kg], reads=[self.dbuf[wname]], writes=[wtb])
                for j in range(n):
                    ps, pb = pss[j]
                    K.mm(ps[:, :TT], [(wt[:, k * 512 + j * 128:k * 512 + (j + 1) * 128], act(kg * kpg + k)) for k in range(kpg)],
                         reads=[wtb] + list(act_bufs), wbuf=pb, start=(kg == 0), stop=(kg == nkg - 1))
            for j in range(n):
                cb(blk, j, pss[j][0], pss[j][1])

    def phase1b(self):
        K, nc, c = self.K, self.nc, self.cfg
        self.K.barrier()
        with ExitStack() as es:
            self.psum = RPool(K, es, "psb", 8, [128, 512], F32, psum=True)
            self.ybp = RPool(K, es, "b_yb", 6, [128, 512], BF16)
            self.tfp = RPool(K, es, "b_tf", 4, [128, 512], F32)
            self.rsp = RPool(K, es, "b_rs", 2, [128, 512], F32)
            hp = RPool(K, es, "b_hT", 2, [128, KC, 512], BF16)
            rtp = RPool(K, es, "b_rt", 2, [128, 4, 512], F32)
            wpool = RPool(K, es, "b_w", 2, [128, KC * 512], BF16)
            h3 = self.dram["hT"].rearrange("(k p) a -> p k a", p=128)
            for (a0, TT, kind) in c.qtiles:
                hT, hTb = hp.get()
                rt, rtb = rtp.get()
                K.dma_in(hT[:, :, :TT], h3[:, :, a0:a0 + TT], reads=[self.dbuf["hT"]], writes=[hTb])
                for i, nm in enumerate(("cos64", "sin64", "cos128", "sin128")):
                    K.dma_in(rt[:, i, :TT], self.dram[nm][:, a0:a0 + TT], writes=[rtb])

                def cb(blk, j, ps, pb, TT=TT, a0=a0, rt=rt, rtb=rtb):
                    ch = blk * 4 + j
                    if ch < 2:
                        self.ep_copy(ps, pb, TT, "mqT", ch * 128, a0)
                    elif ch < 4:
                        self.ep_copy(ps, pb, TT, "mkT", (ch - 2) * 128, a0, scale=0.125)
                    elif ch < 8:
                        self.ep_copy(ps, pb, TT, "soT", (ch - 4) * 128, a0, func=AF.Sigmoid)
                    elif ch < 12:
                        self.ep_rope(ps, pb, TT, rt, rtb, 64, "dqT", (ch - 8) * 128, a0)
                    elif ch < 16:
                        self.ep_rope(ps, pb, TT, rt, rtb, 64, "sqT", (ch - 12) * 128, a0)
                    else:
                        self.ep_norm_rope(ps, pb, TT, self.gqg, rt, rtb, "gqT", (ch - 16) * 128, a0)

                self.fm_gemm(lambda k, hT=hT, TT=TT: hT[:, k, :TT], [hTb], TT, "wB", range(5), 1, KC, cb, wpool)


def _pk(v):
    return np.ascontiguousarray(np.asarray(v, np.float32).reshape(-1, 128).T)


def rope_tables(pos, valid, d):
    nf = d // 4
    inv = (np.float32(10000.0) ** (-(np.arange(nf, dtype=np.float32) / np.float32(nf)))).astype(np.float32)
    row = (pos // GRID_W).astype(np.float32)
    col = (pos % GRID_W).astype(np.float32)
    p = np.arange(128)
    j = p % d
    axis = j // (d // 2)
    f = j % nf
    ang = np.where(axis[:, None] == 0, row[None, :], col[None, :]).astype(np.float32) * inv[f][:, None]
    cos = np.cos(ang).astype(np.float32)
    sin = np.sin(ang).astype(np.float32)
    cos[:, ~valid] = 1.0
    sin[:, ~valid] = 0.0
    return np.ascontiguousarray(cos), np.ascontiguousarray(sin)


def rot_mats():
    R = np.zeros((128, 2, 128), np.float32)
    for idx, d in enumerate((64, 128)):
        q = d // 4
        for pp in range(128):
            j = pp % d
            half = (j % (d // 2)) // q
            if half == 0:
                R[pp + q, idx, pp] = -1.0
            else:
                R[pp - q, idx, pp] = 1.0
    return R


def swa_masks(r, nr):
    m = np.zeros((128, 8, 512), np.float32)
    kl = np.arange(128)[:, None]
    ql = np.arange(512)[None, :]
    for i, o in enumerate(range(-1, 5)):
        m[:, i, :] = (np.abs(128 * o + kl - ql) <= 128)
    m[:, 6, :] = m[:, 0, :] * (1.0 if r > 0 else 0.0)
    m[:, 7, :] = m[:, 5, :] * (1.0 if r < nr - 1 else 0.0)
    return m


def prep_core_inputs(inputs, l, x_in, ctx_in, b, r, cfg):
    N, NO = cfg.N, cfg.NO
    o0, o1 = r * NO, (r + 1) * NO
    xb = x_in[b]
    halo = np.zeros((256, D), np.float32)
    if o0 >= 128:
        halo[0:128] = xb[o0 - 128:o0]
    if o1 + 128 <= N:
        halo[128:256] = xb[o1:o1 + 128]
    xa = np.concatenate([ctx_in[b], xb[o0:o1], xb[:o0], xb[o1:], halo], axis=0)
    xT = np.ascontiguousarray(xa.T)
    pos = np.concatenate([np.zeros(NCTX, np.int64), np.arange(o0, o1), np.arange(0, o0), np.arange(o1, N),
                          np.arange(o0 - 128, o0), np.arange(o1, o1 + 128)])
    valid = np.ones(cfg.NA, bool)
    valid[:NCTX] = False
    valid[cfg.HALO0:] = (pos[cfg.HALO0:] >= 0) & (pos[cfg.HALO0:] < N)
    pos = np.clip(pos, 0, N - 1)
    c64, s64 = rope_tables(pos, valid, 64)
    c128, s128 = rope_tables(pos, valid, 128)
    f32 = np.float32
    cvec = np.stack([_pk(inputs["c"][b]), _pk(inputs["c_ctx"])], axis=-1)
    ab = _pk(inputs["ada_b"][l])
    nch = cfg.NOT // 128
    fl = (np.arange(nch) < (o0 // 128)).astype(f32)
    mflags = np.broadcast_to(np.stack([fl, 1.0 - fl], 0)[None], (128, 2, nch))
    rep = lambda a, n: np.ascontiguousarray(np.broadcast_to(np.asarray(a, f32).reshape(1, -1), (128, n)))
    m = {
        "xT": xT,
        "cvec": np.ascontiguousarray(cvec),
        "ada_w": np.ascontiguousarray(inputs["ada_w"][l]),
        "ada_b": np.ascontiguousarray(np.stack([ab, ab], -1)),
        "n1g": np.ascontiguousarray(np.stack([_pk(inputs["norm1_g"][l])] * 2, -1)),
        "n2g": np.ascontiguousarray(np.stack([_pk(inputs["norm2_g"][l])] * 2, -1)),
        "fng": _pk(inputs["final_norm_g"]),
        "w_in": np.ascontiguousarray(inputs["w_in"][l]),
        "gate_b": rep(inputs["mlstm_gate_b"][l].reshape(-1), 16),
        "mng": np.ascontiguousarray(np.asarray(inputs["mlstm_norm_g"][l], f32).reshape(4, 128).T),
        "dlam": np.ascontiguousarray(np.broadcast_to(np.asarray(inputs["diff_lambda"][l], f32)[None], (128, 4, 4, 64))),
        "dng": np.ascontiguousarray(np.asarray(inputs["diff_norm_g"][l], f32).reshape(4, 128).T),
        "sink": rep(inputs["swa_sink"][l], 8),
        "gqg": np.ascontiguousarray(np.asarray(inputs["gqa_q_norm_g"][l], f32).reshape(128, 1)),
        "gkg": np.ascontiguousarray(np.asarray(inputs["gqa_k_norm_g"][l], f32).reshape(128, 1)),
        "w_branch": np.ascontiguousarray(inputs["w_branch"][l]),
        "w_out": np.ascontiguousarray(inputs["w_out"][l]),
        "w_up": np.ascontiguousarray(inputs["w_up"][l]),
        "w_down": np.ascontiguousarray(inputs["w_down"][l]),
        "cos64": c64, "sin64": s64, "cos128": c128, "sin128": s128,
        "rmats": rot_mats(),
        "ident": np.eye(128, dtype=f32),
        "trimask": np.ascontiguousarray(np.stack([np.triu(np.ones((128, 128), f32)), np.tril(np.ones((128, 128), f32))], 1)),
        "swamask": swa_masks(r, 4),
        "mflags": np.ascontiguousarray(mflags, dtype=f32),
    }
    return m


def _attn_methods():
    def phase_attn(self):
        K, nc, c = self.K, self.nc, self.cfg
        K.barrier()
        NKV = c.NKV
        nkb = NKV // 128
        with ExitStack() as es:
            self.psum = RPool(K, es, "pst", 3, [128, 1024], F32, psum=True)
            accs = RPool(K, es, "psacc", 2, [128, 512], F32, psum=True)
            (accO, accOb), (accL, accLb) = accs.tiles
            self.ybp = RPool(K, es, "t_yb", 4, [128, 512], BF16)
            self.tfp = RPool(K, es, "t_tf", 4, [128, 512], F32)
            self.rsp = RPool(K, es, "t_rs", 2, [128, 512], F32)
            pp = RPool(K, es, "t_p", 3, [128, 1024], BF16)
            lap = RPool(K, es, "t_la", 2, [128, 1024], F32)
            qp = RPool(K, es, "t_q", 2, [128, 512], BF16)
            op_ = RPool(K, es, "t_o", 3, [128, 512], F32)
            kT = self.sb(es, "t_kT", [128, NKV], BF16); kTb = Buf()
            vv = self.sb(es, "t_vv", [128, nkb, 128], BF16); vvb = Buf()
            sm = self.sb(es, "t_sm", [128, 8, 512], BF16)
            smf = self.sb(es, "t_smf", [128, 8, 512], F32)
            small = Buf("attn_small")
            K.dma_in(smf[:], self.dram["swamask"], writes=[small])
            K.op(K.dve, lambda: nc.vector.tensor_copy(sm[:], smf[:]), reads=[small], writes=[small])
            dl = self.sb(es, "t_dl", [128, 4, 4, 64], F32)
            K.dma_in(dl[:], self.dram["dlam"], writes=[small])
            pr = self.sb(es, "t_pr", [128, 2, 4, 64], F32)
            K.op(K.dve, lambda: nc.vector.tensor_tensor(out=pr[:, 0], in0=dl[:, 0], in1=dl[:, 1], op=ALU.mult), reads=[small], writes=[small])
            K.op(K.dve, lambda: nc.vector.tensor_tensor(out=pr[:, 1], in0=dl[:, 2], in1=dl[:, 3], op=ALU.mult), reads=[small], writes=[small])
            sums = self.sb(es, "t_sums", [128, 8], F32)
            K.op(K.dve, lambda: nc.vector.reduce_sum(out=sums[:], in_=pr[:].rearrange("p a h d -> p (a h) d"),
                                                     axis=mybir.AxisListType.X), reads=[small], writes=[small])
            K.op(K.act, lambda: nc.scalar.activation(out=sums[:], in_=sums[:], func=AF.Exp), reads=[small], writes=[small])
            nlam = self.sb(es, "t_nlam", [128, 4], F32)
            K.op(K.dve, lambda: nc.vector.tensor_tensor(out=nlam[:], in0=sums[:, 4:8], in1=sums[:, 0:4], op=ALU.subtract),
                 reads=[small], writes=[small])
            K.op(K.dve, lambda: nc.vector.tensor_scalar(out=nlam[:], in0=nlam[:], scalar1=-float(c.lam_init), scalar2=None,
                                                        op0=ALU.add), reads=[small], writes=[small])
            dg = self.sb(es, "t_dg", [128, 4], F32)
            K.dma_in(dg[:], self.dram["dng"], writes=[small])
            K.op(K.dve, lambda: nc.vector.tensor_scalar(out=dg[:], in0=dg[:], scalar1=float(1.0 - c.lam_init), scalar2=None,
                                                        op0=ALU.mult), reads=[small], writes=[small])
            esk = self.sb(es, "t_esk", [128, 8], F32)
            K.dma_in(esk[:], self.dram["sink"], writes=[small])
            K.op(K.act, lambda: nc.scalar.activation(out=esk[:], in_=esk[:], func=AF.Exp), reads=[small], writes=[small])

            def unit(q, qb, p0, p1, TT, blocks, scale, dv):
                n = len(blocks)
                prs = [blocks[i:i + 2] for i in range(0, n, 2)]
                npr = len(prs)
                Ss = {}
                Ps = {}

                def qk(j):
                    S, Sb = self.psum.get()
                    for u, (k_ap, v_ap, m_ap, bufs) in enumerate(prs[j]):
                        K.mm(S[:, u * 512:u * 512 + TT], [(k_ap, q[p0:p1, :TT])], reads=[qb] + bufs, wbuf=Sb)
                    Ss[j] = (S, Sb)

                def ex(j):
                    S, Sb = Ss.pop(j)
                    P, Pb = pp.get()
                    nb = len(prs[j])
                    K.op(K.act, lambda: nc.scalar.activation(out=P[:].rearrange("p (u t) -> p u t", u=2)[:, 0:nb, 0:TT],
                                                             in_=S[:].rearrange("p (u t) -> p u t", u=2)[:, 0:nb, 0:TT],
                                                             func=AF.Exp, scale=scale), reads=[Sb], writes=[Pb])
                    for u, (k_ap, v_ap, m_ap, bufs) in enumerate(prs[j]):
                        if m_ap is not None:
                            K.op(K.pool, lambda: nc.gpsimd.tensor_tensor(out=P[:, u * 512:u * 512 + TT], in0=P[:, u * 512:u * 512 + TT],
                                                                         in1=m_ap[:, :TT], op=ALU.mult), reads=[Pb, small], writes=[Pb])
                    Ps[j] = (P, Pb)

                La, Lab = lap.get()
                K.op(K.dve, lambda: nc.vector.memset(La[:], 0.0), writes=[Lab])

                def pv(j):
                    P, Pb = Ps.pop(j)
                    nb = len(prs[j])
                    for u, (k_ap, v_ap, m_ap, bufs) in enumerate(prs[j]):
                        i = 2 * j + u
                        K.mm(accO[0:dv, :TT], [(v_ap, P[:, u * 512:u * 512 + TT])], reads=[Pb] + bufs, wbuf=accOb,
                             start=(i == 0), stop=(i == n - 1))
                    Lv = La[:].rearrange("p (u t) -> p u t", u=2)[:, 0:nb, 0:TT]
                    Pv = P[:].rearrange("p (u t) -> p u t", u=2)[:, 0:nb, 0:TT]
                    K.op(K.dve, lambda: nc.vector.tensor_tensor(out=Lv, in0=Lv, in1=Pv, op=ALU.add), reads=[Pb, Lab], writes=[Lab])

                qk(0)
                if npr > 1:
                    qk(1)
                for j in range(npr):
                    ex(j)
                    pv(j)
                    if j + 2 < npr:
                        qk(j + 2)
                K.mm(accL[:, :TT], [(self.ones_f[:], La[:, 0:TT]), (self.ones_f[:], La[:, 512:512 + TT])], reads=[Lab, self.cb], wbuf=accLb)

            def recip_L(TT, add_ap=None):
                r, rb = self.rsp.get()
                if add_ap is None:
                    K.op(K.dve, lambda: nc.vector.reciprocal(out=r[:, :TT], in_=accL[:, :TT]), reads=[accLb], writes=[rb])
                else:
                    K.op(K.dve, lambda: nc.vector.tensor_scalar(out=r[:, :TT], in0=accL[:, :TT], scalar1=add_ap, scalar2=None,
                                                                op0=ALU.add), reads=[accLb, small], writes=[rb])
                    K.op(K.dve, lambda: nc.vector.reciprocal(out=r[:, :TT], in_=r[:, :TT]), reads=[rb], writes=[rb])
                return r, rb

            def load_q(name, row0, a0, TT):
                q, qb = qp.get()
                K.dma_in(q[:, :TT], self.dram[name][row0:row0 + 128, a0:a0 + TT], reads=[self.dbuf[name]], writes=[qb])
                return q, qb

            def store_y(src, srcb, br, row0, nrows, a0, TT):
                K.dma_out(self.dram["yT"][br, row0:row0 + nrows, a0:a0 + TT], src, reads=[srcb], writes=[self.dbuf["yT"]])

            qtiles = [t for t in c.qtiles if not (c.last and t[2] == "ctx")]
            kv3 = self.dram["kvTM"][0:NKV, :].rearrange("(b p) c -> p b c", p=128)

            for h in range(4):
                K.dma_in(kT[:], self.dram["dkT"][h * 128:(h + 1) * 128, 0:NKV], reads=[self.dbuf["dkT"]], writes=[kTb])
                for b0 in range(0, nkb, 16):
                    b1 = min(nkb, b0 + 16)
                    K.dma_in(vv[:, b0:b1, :], kv3[:, b0:b1, 1024 + h * 128:1024 + (h + 1) * 128], reads=[self.dbuf["kvTM"]], writes=[vvb])
                for (a0, TT, kind) in qtiles:
                    q, qb = load_q("dqT", h * 128, a0, TT)
                    kbs = range(2) if kind == "ctx" else range(nkb)
                    os_ = []
                    for m in range(2):
                        p0, p1 = 64 * m, 64 * m + 64
                        blocks = [(kT[p0:p1, kb * 128:(kb + 1) * 128], vv[:, kb, :], None, [kTb, vvb]) for kb in kbs]
                        unit(q, qb, p0, p1, TT, blocks, 0.125, 128)
                        r, rb = recip_L(TT)
                        o, ob = op_.get()
                        K.op(K.dve, lambda: nc.vector.tensor_tensor(out=o[:, :TT], in0=accO[:, :TT], in1=r[:, :TT], op=ALU.mult),
                             reads=[accOb, rb], writes=[ob])
                        os_.append((o, ob))
                    (o1, o1b), (o2, o2b) = os_
                    cm, cmb = op_.get()
                    K.op(K.dve, lambda: nc.vector.scalar_tensor_tensor(out=cm[:, :TT], in0=o2[:, :TT], scalar=nlam[:, h:h + 1],
                                                                       in1=o1[:, :TT], op0=ALU.mult, op1=ALU.add),
                         reads=[o1b, o2b, small], writes=[cmb])
                    sq, sqb = self.ybp.get()
                    K.op(K.act, lambda: nc.scalar.activation(out=sq[:, :TT], in_=cm[:, :TT], func=AF.Square), reads=[cmb], writes=[sqb])
                    ssp, sspb = self.psum.get()
                    K.mm(ssp[:, :TT], [(self.ones_bf[:], sq[:, :TT])], reads=[sqb, self.cb], wbuf=sspb)
                    r1, r1b = self.rstd_from_ps(ssp, sspb, TT, 128.0)
                    y, yb = self.ybp.get()
                    K.op(K.dve, lambda: nc.vector.scalar_tensor_tensor(out=y[:, :TT], in0=cm[:, :TT], scalar=dg[:, h:h + 1],
                                                                       in1=r1[:, :TT], op0=ALU.mult, op1=ALU.mult),
                         reads=[cmb, r1b, small], writes=[yb])
                    store_y(y[:, :TT], yb, 1, h * 128, 128, a0, TT)

            for g in range(2):
                K.dma_in(kT[:], self.dram["gkT"][g * 128:(g + 1) * 128, 0:NKV], reads=[self.dbuf["gkT"]], writes=[kTb])
                for b0 in range(0, nkb, 16):
                    b1 = min(nkb, b0 + 16)
                    K.dma_in(vv[:, b0:b1, :], kv3[:, b0:b1, 256 + g * 128:256 + (g + 1) * 128], reads=[self.dbuf["kvTM"]], writes=[vvb])
                for h in (2 * g, 2 * g + 1):
                    for (a0, TT, kind) in qtiles:
                        q, qb = load_q("gqT", h * 128, a0, TT)
                        kbs = range(2) if kind == "ctx" else range(nkb)
                        blocks = [(kT[:, kb * 128:(kb + 1) * 128], vv[:, kb, :], None, [kTb, vvb]) for kb in kbs]
                        unit(q, qb, 0, 128, TT, blocks, 128.0 ** -0.5, 128)
                        r, rb = recip_L(TT)
                        y, yb = self.ybp.get()
                        K.op(K.dve, lambda: nc.vector.tensor_tensor(out=y[:, :TT], in0=accO[:, :TT], in1=r[:, :TT], op=ALU.mult),
                             reads=[accOb, rb], writes=[yb])
                        store_y(y[:, :TT], yb, 3, h * 128, 128, a0, TT)

            NO = c.NO
            nob = NO // 128
            nsk = 2 + nob + 2
            kTs = self.sb(es, "t_kTs", [128, nsk * 128], BF16); kTsb = Buf()
            vs = self.sb(es, "t_vs", [128, nsk, 64], BF16); vsb = Buf()
            kvs3 = lambda r0, n: self.dram["kvTM"][r0:r0 + n * 128, :].rearrange("(b p) c -> p b c", p=128)
            for g in range(2):
                for half in range(2):
                    K.dma_in(kTs[half * 64:(half + 1) * 64, 0:(2 + nob) * 128],
                             self.dram["skT"][g * 64:(g + 1) * 64, 0:(2 + nob) * 128], reads=[self.dbuf["skT"]], writes=[kTsb])
                    K.dma_in(kTs[half * 64:(half + 1) * 64, (2 + nob) * 128:nsk * 128],
                             self.dram["skT"][g * 64:(g + 1) * 64, c.HALO0:c.HALO0 + 256], reads=[self.dbuf["skT"]], writes=[kTsb])
                for b0 in range(0, 2 + nob, 16):
                    b1 = min(2 + nob, b0 + 16)
                    K.dma_in(vs[:, b0:b1, :], kvs3(0, 2 + nob)[:, b0:b1, 1536 + g * 64:1536 + (g + 1) * 64],
                             reads=[self.dbuf["kvTM"]], writes=[vsb])
                K.dma_in(vs[:, 2 + nob:nsk, :], kvs3(c.HALO0, 2)[:, :, 1536 + g * 64:1536 + (g + 1) * 64],
                         reads=[self.dbuf["kvTM"]], writes=[vsb])
                for h in range(4 * g, 4 * g + 4):
                    p0 = (h % 2) * 64
                    for (a0, TT, kind) in qtiles:
                        q, qb = load_q("sqT", (h // 2) * 128, a0, TT)
                        bl = [(0, None), (1, None)]
                        if kind == "own":
                            i = (a0 - c.OWN0) // 512
                            for oi, o in enumerate(range(-1, 5)):
                                ob_ = 4 * i + o
                                if ob_ < 0:
                                    bl.append((2 + nob, 6))
                                elif ob_ >= nob:
                                    bl.append((2 + nob + 1, 7))
                                else:
                                    bl.append((2 + ob_, oi))
                        blocks = [(kTs[p0:p0 + 64, b * 128:(b + 1) * 128], vs[:, b, :], None if mi is None else sm[:, mi, :],
                                   [kTsb, vsb]) for (b, mi) in bl]
                        unit(q, qb, p0, p0 + 64, TT, blocks, 0.125, 64)
                        r, rb = recip_L(TT, add_ap=esk[:, h:h + 1])
                        y, yb = self.ybp.get()
                        K.op(K.dve, lambda: nc.vector.tensor_tensor(out=y[0:64, :TT], in0=accO[0:64, :TT], in1=r[0:64, :TT], op=ALU.mult),
                             reads=[accOb, rb], writes=[yb])
                        store_y(y[0:64, :TT], yb, 2, h * 64, 64, a0, TT)

    Prog.phase_attn = phase_attn


_attn_methods()


def _ffn_methods():
    def phase_merge_ffn(self):
        K, nc, c = self.K, self.nc, self.cfg
        K.barrier()
        with ExitStack() as es:
            self.psum = RPool(K, es, "psf", 8, [128, 512], F32, psum=True)
            self.tfp = RPool(K, es, "f_tf", 4, [128, 512], F32)
            self.rsp = RPool(K, es, "f_rs", 2, [128, 512], F32)
            wpool = RPool(K, es, "f_w", 2, [128, KC * 512], BF16)
            xt = self.sb(es, "f_xt", [128, KC, 512], F32); xtb = Buf()
            hT = self.sb(es, "f_hT", [128, KC, 512], BF16); hTb = Buf()
            ysb = self.sb(es, "f_y", [128, KC, 512], BF16); ysbb = Buf()
            accT = self.sb(es, "f_acc", [128, KC, 512], BF16); accTb = Buf()
            aT = self.sb(es, "f_aT", [128, 44, 512], BF16); aTb = Buf()
            accf = self.sb(es, "f_accf", [128, 4, 512], F32); accfb = [Buf() for _ in range(4)]
            sg = self.sb(es, "f_sg", [128, 4, 512], F32); sgb = [Buf() for _ in range(4)]
            fng = self.sb(es, "f_fng", [128, KC], F32)
            K.dma_in(fng[:], self.dram["fng"], writes=[self.cb])
            x3 = self.dram["xT"].rearrange("(k p) a -> p k a", p=128)
            h3 = self.dram["hT"].rearrange("(k p) a -> p k a", p=128)
            y3 = self.dram["yT"].rearrange("i (k p) a -> p (i k) a", p=128)
            qtiles = [t for t in c.qtiles if not (c.last and t[2] == "ctx")]
            for (a0, TT, kind) in qtiles:
                v = 1 if kind == "ctx" else 0
                K.dma_in(xt[:, :, :TT], x3[:, :, a0:a0 + TT], writes=[xtb])
                K.dma_in(hT[:, :, :TT], h3[:, :, a0:a0 + TT], reads=[self.dbuf["hT"]], writes=[hTb])
                K.dma_in(ysb[:, :, :TT], y3[:, :, a0:a0 + TT], reads=[self.dbuf["yT"]], writes=[ysbb])
                for cbk in range(4):
                    for i in range(4):
                        def cb_gate(blk, j, ps, pb):
                            K.op(K.act, lambda: nc.scalar.activation(out=sg[:, j, :TT], in_=ps[:, :TT], func=AF.Sigmoid),
                                 reads=[pb], writes=[sgb[j]])

                        def cb_proj(blk, j, ps, pb, i=i):
                            if i == 0:
                                K.op(K.dve, lambda: nc.vector.tensor_tensor(out=accf[:, j, :TT], in0=ps[:, :TT], in1=sg[:, j, :TT],
                                                                            op=ALU.mult), reads=[pb, sgb[j]], writes=[accfb[j]])
                            else:
                                t, tb = self.tfp.get()
                                K.op(K.dve, lambda: nc.vector.tensor_tensor(out=t[:, :TT], in0=ps[:, :TT], in1=sg[:, j, :TT],
                                                                            op=ALU.mult), reads=[pb, sgb[j]], writes=[tb])
                                K.op(K.pool, lambda: nc.gpsimd.tensor_tensor(out=accf[:, j, :TT], in0=accf[:, j, :TT], in1=t[:, :TT],
                                                                             op=ALU.add), reads=[tb, accfb[j]], writes=[accfb[j]])

                        self.fm_gemm(lambda k: hT[:, k, :TT], [hTb], TT, "wG", [i * 4 + cbk], 1, KC, cb_gate, wpool)
                        wsm = RPoolView(wpool, 4 * 512)
                        self.fm_gemm(lambda k, i=i: ysb[:, i * 4 + k, :TT], [ysbb], TT, "wBr", [i * 4 + cbk], 1, 4, cb_proj, wpool)
                    for j in range(4):
                        K.op(K.act, lambda: nc.scalar.copy(accT[:, cbk * 4 + j, :TT], accf[:, j, :TT]), reads=[accfb[j]], writes=[accTb])

                def cb_out(blk, j, ps, pb, v=v):
                    cc = blk * 4 + j
                    K.op(K.dve, lambda: nc.vector.scalar_tensor_tensor(out=xt[:, cc, :TT], in0=ps[:, :TT], scalar=self.G1[:, cc, v:v + 1],
                                                                       in1=xt[:, cc, :TT], op0=ALU.mult, op1=ALU.add),
                         reads=[pb, xtb, self.cb], writes=[xtb])

                self.fm_gemm(lambda k: accT[:, k, :TT], [accTb], TT, "wO", range(4), 1, KC, cb_out, wpool)
                self.norm_mod(es, xt, xtb, ysb, ysbb, hT, hTb, TT, self.A2, self.B2, v)
                for bk in range(11):
                    def cb_g(blk, j, ps, pb):
                        K.op(K.act, lambda: nc.scalar.activation(out=sg[:, j, :TT], in_=ps[:, :TT], func=AF.Silu),
                             reads=[pb], writes=[sgb[j]])

                    def cb_u(blk, j, ps, pb, bk=bk):
                        K.op(K.dve, lambda: nc.vector.tensor_tensor(out=aT[:, bk * 4 + j, :TT], in0=ps[:, :TT], in1=sg[:, j, :TT],
                                                                    op=ALU.mult), reads=[pb, sgb[j]], writes=[aTb])

                    self.fm_gemm(lambda k: hT[:, k, :TT], [hTb], TT, "wU", [bk], 1, KC, cb_g, wpool)
                    self.fm_gemm(lambda k: hT[:, k, :TT], [hTb], TT, "wU", [11 + bk], 1, KC, cb_u, wpool)

                def cb_down(blk, j, ps, pb, v=v):
                    cc = blk * 4 + j
                    K.op(K.dve, lambda: nc.vector.scalar_tensor_tensor(out=xt[:, cc, :TT], in0=ps[:, :TT], scalar=self.G2[:, cc, v:v + 1],
                                                                       in1=xt[:, cc, :TT], op0=ALU.mult, op1=ALU.add),
                         reads=[pb, xtb, self.cb], writes=[xtb])

                self.fm_gemm(lambda k: aT[:, k, :TT], [aTb], TT, "wD", range(4), 4, 11, cb_down, wpool)
                if c.last:
                    K.op(K.act, lambda: nc.scalar.activation(out=ysb[:, :, :TT], in_=xt[:, :, :TT], func=AF.Square),
                         reads=[xtb], writes=[ysbb])
                    ssp, sspb = self.psum.get()
                    K.mm(ssp[:, :TT], [(self.ones_bf[:], ysb[:, kc, :TT]) for kc in range(KC)], reads=[ysbb, self.cb], wbuf=sspb)
                    r1, r1b = self.rstd_from_ps(ssp, sspb, TT, float(D))
                    for kc in range(KC):
                        K.op(K.dve, lambda: nc.vector.scalar_tensor_tensor(out=xt[:, kc, :TT], in0=xt[:, kc, :TT], scalar=fng[:, kc:kc + 1],
                                                                           in1=r1[:, :TT], op0=ALU.mult, op1=ALU.mult),
                             reads=[xtb, r1b, self.cb], writes=[xtb])
                o3 = self.dram["outT"].rearrange("(k p) a -> p k a", p=128)
                if kind == "ctx":
                    c3 = self.dram["ctxoT"].rearrange("(k p) a -> p k a", p=128)
                    K.dma_out(c3[:, :, 0:TT], xt[:, :, :TT], reads=[xtb], writes=[self.dbuf["ctxoT"]])
                else:
                    K.dma_out(o3[:, :, a0 - c.OWN0:a0 - c.OWN0 + TT], xt[:, :, :TT], reads=[xtb], writes=[self.dbuf["outT"]])

    Prog.phase_merge_ffn = phase_merge_ffn


def RPoolView(pool, n):
    return pool


_ffn_methods()


def _mlstm_methods():
    def phase_mlstm(self):
        K, nc, c = self.K, self.nc, self.cfg
        K.barrier()
        C = c.NKV // 128
        NOc = c.NO // 128
        NOTc = c.NOT // 128
        OTH = 2 + NOc
        with ExitStack() as es:
            self.psum = RPool(K, es, "psl", 8, [128, 512], F32, psum=True)
            self.rsp = RPool(K, es, "l_rs", 2, [128, 512], F32)
            gb = Buf("gates")
            gt = self.sb(es, "l_gt", [128, C, 16], F32)
            g3 = self.dram["gatesTM"][0:c.NKV, :].rearrange("(c p) j -> p c j", p=128)
            for b0 in range(0, C, 16):
                b1 = min(C, b0 + 16)
                K.dma_in(gt[:, b0:b1, :], g3[:, b0:b1, :], reads=[self.dbuf["gatesTM"]], writes=[gb])
            gt4 = gt[:].rearrange("p c (d k) -> p c d k", d=2)
            sp = self.sb(es, "l_sp", [128, C, 2, 4], F32)
            cs = self.sb(es, "l_cs", [128, C, 2, 4], F32)
            tot = self.sb(es, "l_tot", [128, C, 2, 4], F32)
            A = self.sb(es, "l_A", [128, C, 2, 4], F32)
            E = self.sb(es, "l_E", [128, C, 2, 4], F32)
            DEC = self.sb(es, "l_DEC", [128, C, 2, 4], F32)
            W = self.sb(es, "l_W", [128, C, 2, 4], F32)
            tm = self.sb(es, "l_tm", [128, 2, 128], F32)
            mf = self.sb(es, "l_mf", [128, 2, NOTc], F32)
            mng = self.sb(es, "l_mng", [128, 4], F32)
            K.dma_in(tm[:], self.dram["trimask"], writes=[gb])
            K.dma_in(mf[:], self.dram["mflags"], writes=[gb])
            K.dma_in(mng[:], self.dram["mng"], writes=[gb])
            K.op(K.act, lambda: nc.scalar.activation(out=sp[:], in_=gt4[:, :, :, 4:8], func=AF.Exp, scale=-1.0), reads=[gb], writes=[gb])
            K.op(K.act, lambda: nc.scalar.activation(out=sp[:], in_=sp[:], func=AF.Ln, bias=self.ones_f[:, 0:1], scale=1.0),
                 reads=[gb, self.cb], writes=[gb])
            for c0 in range(0, C, 64):
                cc = min(64, C - c0)
                for d in range(2):
                    ps, pb = self.psum.get()
                    K.mm(ps[:, 0:cc * 4].rearrange("p (c k) -> p c k", k=4), [(tm[:, d, :], sp[:, c0:c0 + cc, d, :])], reads=[gb], wbuf=pb)
                    K.op(K.dve, lambda: nc.vector.tensor_copy(cs[:, c0:c0 + cc, d, :], ps[:, 0:cc * 4].rearrange("p (c k) -> p c k", k=4)),
                         reads=[pb], writes=[gb])
                ps, pb = self.psum.get()
                K.mm(ps[:, 0:cc * 8], [(self.ones_f[:], sp[:, c0:c0 + cc].rearrange("p c d k -> p (c d k)"))], reads=[gb, self.cb], wbuf=pb)
                K.op(K.dve, lambda: nc.vector.tensor_copy(tot[:, c0:c0 + cc].rearrange("p c d k -> p (c d k)"), ps[:, 0:cc * 8]),
                     reads=[pb], writes=[gb])
            K.op(K.dve, lambda: nc.vector.tensor_tensor(out=A[:], in0=gt4[:, :, :, 0:4], in1=cs[:], op=ALU.add), reads=[gb], writes=[gb])
            K.op(K.act, lambda: nc.scalar.activation(out=A[:], in_=A[:], func=AF.Exp), reads=[gb], writes=[gb])
            K.op(K.act, lambda: nc.scalar.activation(out=E[:], in_=cs[:], func=AF.Exp), reads=[gb], writes=[gb])
            K.op(K.act, lambda: nc.scalar.activation(out=DEC[:], in_=tot[:], func=AF.Exp, scale=-1.0), reads=[gb], writes=[gb])
            K.op(K.dve, lambda: nc.vector.tensor_tensor(out=W[:], in0=A[:], in1=DEC[:], op=ALU.mult), reads=[gb], writes=[gb])
            for d in range(2):
                for h in range(4):
                    K.op(K.dve, lambda: nc.vector.tensor_tensor(out=W[:, OTH:C, d, h], in0=W[:, OTH:C, d, h], in1=mf[:, d, :], op=ALU.mult),
                         reads=[gb], writes=[gb])
                    K.op(K.dve, lambda: nc.vector.scalar_tensor_tensor(out=DEC[:, OTH:C, d, h], in0=DEC[:, OTH:C, d, h], scalar=-1.0,
                                                                       in1=mf[:, d, :], op0=ALU.add, op1=ALU.mult), reads=[gb], writes=[gb])
                    K.op(K.dve, lambda: nc.vector.tensor_scalar(out=DEC[:, OTH:C, d, h], in0=DEC[:, OTH:C, d, h], scalar1=1.0, scalar2=None,
                                                                op0=ALU.add), reads=[gb], writes=[gb])
            hf = self.sb(es, "l_hf", [128, 4, c.NOC], BF16); hfb = Buf()
            qcp = RPool(K, es, "l_qc", 3, [64, 2, 4, 128], BF16)
            sop = RPool(K, es, "l_so", 3, [128, 4, 128], BF16)
            mq3 = self.dram["mqT"].rearrange("(h p) a -> p h a", p=64)
            mk3 = self.dram["mkT"].rearrange("(h p) a -> p h a", p=64)
            so3 = self.dram["soT"].rearrange("(h p) a -> p h a", p=128)
            st = self.sb(es, "l_st", [64, 8, 256], F32)
            stb = self.sb(es, "l_stb", [64, 8, 256], BF16)
            stbuf = [Buf() for _ in range(8)]
            kp = RPool(K, es, "l_k", 3, [128, 256], BF16)
            vp = RPool(K, es, "l_v", 3, [128, 4, 256], BF16)
            for (t, b) in vp.tiles:
                K.op(K.pool, lambda: nc.gpsimd.memset(t[:, :, 128:256], 1.0), writes=[b])
            b16 = RPool(K, es, "l_b16", 6, [128, 256], BF16)
            f32p = RPool(K, es, "l_f32", 6, [128, 128], F32)
            ybp = RPool(K, es, "l_y", 3, [128, 128], BF16)

            def run_dir(d):
                for i in range(8):
                    if i // 4 == d:
                        K.op(K.dve, lambda: nc.vector.memset(st[:, i, :], 0.0), writes=[stbuf[i]])
                        K.op(K.pool, lambda: nc.gpsimd.memset(stb[:, i, :], 0.0), writes=[stbuf[i]])
                others = [OTH + j for j in range(NOTc)]
                own = [2 + j for j in range(NOc)]
                if d == 0:
                    seq = [(0, not c.last), (1, not c.last)] + [(x, False) for x in others] + [(x, True) for x in own]
                else:
                    seq = [(1, not c.last), (0, not c.last)] + [(x, False) for x in reversed(others)] + [(x, True) for x in reversed(own)]
                for si, (ci, full) in enumerate(seq):
                    kch, kchb = kp.get()
                    va, vab = vp.get()
                    K.dma_in(kch[:], self.dram["kvTM"][ci * 128:(ci + 1) * 128, 0:256], reads=[self.dbuf["kvTM"]], writes=[kchb])
                    K.dma_in(va[:, :, 0:128], self.dram["kvTM"][ci * 128:(ci + 1) * 128, 512:1024].rearrange("p (h e) -> p h e", h=4),
                             reads=[self.dbuf["kvTM"]], writes=[vab])
                    col = ci * 128
                    if full:
                        qc, qb = qcp.get()
                        K.dma_in(qc[:, 0], mq3[:, :, col:col + 128], reads=[self.dbuf["mqT"]], writes=[qb])
                        K.dma_in(qc[:, 1], mk3[:, :, col:col + 128], reads=[self.dbuf["mkT"]], writes=[qb])
                        if d == 1:
                            soc, socb = sop.get()
                            K.dma_in(soc[:], so3[:, :, col:col + 128], reads=[self.dbuf["soT"]], writes=[socb])
                    for h in range(4):
                        sidx = d * 4 + h
                        sb_ = stbuf[sidx]
                        if full:
                            S, Sb = self.psum.get()
                            K.mm(S[:, 0:128], [(qc[:, 1, h, :], qc[:, 0, h, :])], reads=[qb], wbuf=Sb)
                            Sm, Smb = b16.get()
                            K.op(K.dve, lambda: nc.vector.tensor_tensor(out=Sm[:, 0:128], in0=S[:, 0:128], in1=tm[:, d, :], op=ALU.mult),
                                 reads=[Sb, gb], writes=[Smb])
                            vA, vAb = b16.get()
                            K.op(K.dve, lambda: nc.vector.tensor_scalar(out=vA[:], in0=va[:, h, :], scalar1=A[:, ci, d, h:h + 1], scalar2=None,
                                                                        op0=ALU.mult), reads=[vab, gb], writes=[vAb])
                            num, numb = self.psum.get()
                            K.mm(num[:, 0:128], [(vA[:, 0:128], Sm[:, 0:128]), (stb[:, sidx, 0:128], qc[:, 0, h, :])],
                                 reads=[vAb, Smb, sb_, qb], wbuf=numb)
                            den, denb = self.psum.get()
                            K.mm(den[:, 0:128], [(vA[:, 128:256], Sm[:, 0:128]), (stb[:, sidx, 128:256], qc[:, 0, h, :])],
                                 reads=[vAb, Smb, sb_, qb], wbuf=denb)
                            ie, ieb = b16.get()
                            K.op(K.dve, lambda: nc.vector.tensor_scalar(out=ie[:, 0:128], in0=self.ident_bf[:], scalar1=E[:, ci, d, h:h + 1],
                                                                        scalar2=None, op0=ALU.mult), reads=[gb, self.cb], writes=[ieb])
                            eb, ebb = self.psum.get()
                            K.mm(eb[:, 0:128], [(self.ones_bf[:], ie[:, 0:128])], reads=[ieb, self.cb], wbuf=ebb)
                            ebs, ebsb = f32p.get()
                            K.op(K.act, lambda: nc.scalar.copy(ebs[:], eb[:, 0:128]), reads=[ebb], writes=[ebsb])
                            mx, mxb = f32p.get()
                            K.op(K.act, lambda: nc.scalar.activation(out=mx[:], in_=den[:, 0:128], func=AF.Abs), reads=[denb], writes=[mxb])
                            K.op(K.dve, lambda: nc.vector.tensor_tensor(out=mx[:], in0=mx[:], in1=ebs[:], op=ALU.max),
                                 reads=[mxb, ebsb], writes=[mxb])
                            K.op(K.dve, lambda: nc.vector.reciprocal(out=mx[:], in_=mx[:]), reads=[mxb], writes=[mxb])
                            if d == 0:
                                K.op(K.dve, lambda: nc.vector.tensor_tensor(out=hf[:, h, col:col + 128], in0=num[:, 0:128], in1=mx[:], op=ALU.mult),
                                     reads=[numb, mxb], writes=[hfb])
                            else:
                                hs, hsb = f32p.get()
                                K.op(K.dve, lambda: nc.vector.tensor_tensor(out=hs[:], in0=num[:, 0:128], in1=mx[:], op=ALU.mult),
                                     reads=[numb, mxb], writes=[hsb])
                                K.op(K.pool, lambda: nc.gpsimd.tensor_tensor(out=hs[:], in0=hs[:], in1=hf[:, h, col:col + 128], op=ALU.add),
                                     reads=[hsb, hfb], writes=[hsb])
                                sq, sqb = b16.get()
                                K.op(K.act, lambda: nc.scalar.activation(out=sq[:, 0:128], in_=hs[:], func=AF.Square), reads=[hsb], writes=[sqb])
                                ssp, sspb = self.psum.get()
                                K.mm(ssp[:, 0:128], [(self.ones_bf[:], sq[:, 0:128])], reads=[sqb, self.cb], wbuf=sspb)
                                r1, r1b = self.rstd_from_ps(ssp, sspb, 128, 128.0)
                                K.op(K.dve, lambda: nc.vector.scalar_tensor_tensor(out=hs[:], in0=hs[:], scalar=mng[:, h:h + 1], in1=r1[:, 0:128],
                                                                                   op0=ALU.mult, op1=ALU.mult), reads=[hsb, r1b, gb], writes=[hsb])
                                y, yb = ybp.get()
                                K.op(K.dve, lambda: nc.vector.tensor_tensor(out=y[:], in0=hs[:], in1=soc[:, h, :], op=ALU.mult),
                                     reads=[hsb, socb], writes=[yb])
                                K.dma_out(self.dram["yT"][0, h * 128:(h + 1) * 128, col:col + 128], y[:], reads=[yb], writes=[self.dbuf["yT"]])
                        if si == len(seq) - 1:
                            continue
                        rU, rUb = b16.get()
                        K.op(K.dve, lambda: nc.vector.tensor_scalar(out=rU[:], in0=va[:, h, :], scalar1=W[:, ci, d, h:h + 1], scalar2=None,
                                                                    op0=ALU.mult), reads=[vab, gb], writes=[rUb])
                        U, Ub = self.psum.get()
                        K.mm(U[0:64, 0:256], [(kch[:, h * 64:(h + 1) * 64], rU[:])], reads=[kchb, rUb], wbuf=Ub)
                        K.op(K.dve, lambda: nc.vector.scalar_tensor_tensor(out=st[:, sidx, :], in0=st[:, sidx, :], scalar=DEC[0:64, ci, d, h:h + 1],
                                                                           in1=U[0:64, 0:256], op0=ALU.mult, op1=ALU.add),
                             reads=[Ub, gb, sb_], writes=[sb_])
                        K.op(K.act, lambda: nc.scalar.copy(stb[:, sidx, :], st[:, sidx, :]), reads=[sb_], writes=[sb_])

            run_dir(0)
            run_dir(1)

    Prog.phase_mlstm = phase_mlstm


_mlstm_methods()


def build_program(cfg):
    P = Prog(cfg)
    P.declare()
    with ExitStack() as es:
        P.consts(es)
        P.cast_weights()
        P.phase_mod()
        P.phase1a()
        P.phase1b()
        P.phase_mlstm()
        P.phase_attn()
        P.phase_merge_ffn()
        P.K.finish()
    return P


_PROGS = {}


def _get_prog(N, last, lam_init):
    key = (N, last)
    if key not in _PROGS:
        _PROGS[key] = build_program(Cfg(N=N, last=last, lam_init=lam_init))
    return _PROGS[key]


def kernel(**inputs):
    inputs = {k: np.asarray(v) for k, v in inputs.items()}
    x = np.asarray(inputs["x"], np.float32)
    B, N, _ = x.shape
    xc = np.asarray(inputs["ctx"], np.float32)
    NO = N // 4
    cur = x
    for l in range(DEPTH):
        last = l == DEPTH - 1
        lam_init = 0.8 - 0.6 * math.exp(-0.3 * l)
        P = _get_prog(N, last, lam_init)
        maps = [prep_core_inputs(inputs, l, cur, xc, b, r, P.cfg) for b in range(B) for r in range(4)]
        res = run_bass_kernel_spmd(P.nc, maps, core_ids=list(range(8)))
        nxt = np.empty_like(cur)
        for b in range(B):
            for r in range(4):
                nxt[b, r * NO:(r + 1) * NO] = np.asarray(res.results[b * 4 + r]["outT"]).T
        if not last:
            xc = np.stack([np.asarray(res.results[b * 4]["ctxoT"]).T for b in range(B)], 0).astype(np.float32)
        cur = nxt
    return cur.astype(np.float32)
```

```python
import math
from contextlib import ExitStack

import numpy as np
import concourse.bass as bass
import concourse.mybir as mybir
from concourse.bass_utils import run_bass_kernel_spmd

F32 = mybir.dt.float32
BF16 = mybir.dt.bfloat16
AF = mybir.ActivationFunctionType
ALU = mybir.AluOpType

D = 2048
KC = 16
NCTX = 256
GRID_W = 64
EPS = 1e-6
FFN = 5632
DEPTH = 2


def _merge(d, t):
    for k, (s, v) in t.items():
        if k not in d or d[k][1] < v:
            d[k] = (s, v)


class Buf:
    __slots__ = ("w", "r", "name")

    def __init__(self, name=""):
        self.w = {}
        self.r = {}
        self.name = name


class EngQ:
    ROT = 30000

    def __init__(self, K, name, eng, skip_self=False):
        self.K = K
        self.name = name
        self.eng = eng
        self.seen = {}
        self.skip_self = skip_self
        self.own = set()
        self._new_sem()

    def _new_sem(self):
        self.sem = self.K.nc.alloc_semaphore(f"q{self.name}{self.K.nsem}")
        self.K.nsem += 1
        self.cnt = 0
        self.own.add(id(self.sem))

    def wait(self, deps):
        for key, (sem, val) in deps.items():
            if self.skip_self and key in self.own:
                continue
            if self.seen.get(key, 0) < val:
                self.eng.wait_ge(sem, val)
                self.seen[key] = val

    def emit(self, ins):
        if self.cnt >= self.ROT:
            self._new_sem()
        self.cnt += 1
        ins.then_inc(self.sem, 1)
        return {id(self.sem): (self.sem, self.cnt)}


class DmaQ:
    NS = 16

    def __init__(self, K, name, eng):
        self.K = K
        self.name = name
        self.eng = eng
        self.seen = {}
        self.sems = [K.nc.alloc_semaphore(f"d{name}{i}") for i in range(self.NS)]
        K.nsem += self.NS
        self.cnts = [0] * self.NS
        self.i = 0

    def wait(self, deps):
        for key, (sem, val) in deps.items():
            if self.seen.get(key, 0) < val:
                self.eng.wait_ge(sem, val)
                self.seen[key] = val

    def pre(self):
        j = self.i % self.NS
        if self.cnts[j] > 0:
            key = id(self.sems[j])
            val = 16 * self.cnts[j]
            if self.seen.get(key, 0) < val:
                self.eng.wait_ge(self.sems[j], val)
                self.seen[key] = val

    def emit(self, ins):
        j = self.i % self.NS
        self.i += 1
        if self.cnts[j] >= 30000:
            self.sems[j] = self.K.nc.alloc_semaphore(f"d{self.name}{self.K.nsem}")
            self.K.nsem += 1
            self.cnts[j] = 0
        self.cnts[j] += 1
        ins.then_inc(self.sems[j], 16)
        return {id(self.sems[j]): (self.sems[j], 16 * self.cnts[j])}


class RPool:
    def __init__(self, K, es, name, n, shape, dtype, psum=False):
        self.tiles = []
        for i in range(n):
            if psum:
                t = es.enter_context(K.nc.psum_tensor(f"p_{name}{i}", shape, dtype))
            else:
                t = es.enter_context(K.nc.sbuf_tensor(f"s_{name}{i}", shape, dtype))
            self.tiles.append((t, Buf(f"{name}{i}")))
        self.i = 0

    def get(self):
        t = self.tiles[self.i % len(self.tiles)]
        self.i += 1
        return t


class Ker:
    def __init__(self, nc):
        self.nc = nc
        self.nsem = 0
        self.pe = EngQ(self, "pe", nc.tensor, skip_self=True)
        self.act = EngQ(self, "act", nc.scalar)
        self.dve = EngQ(self, "dve", nc.vector)
        self.pool = EngQ(self, "pool", nc.gpsimd)
        self.ld = DmaQ(self, "ld", nc.sync)
        self.st = DmaQ(self, "st", nc.gpsimd)
        self.st.seen = self.pool.seen
        self.final = {}

    def op(self, q, fn, reads=(), writes=()):
        deps = {}
        for b in reads:
            _merge(deps, b.w)
        for b in writes:
            _merge(deps, b.w)
            _merge(deps, b.r)
        q.wait(deps)
        if isinstance(q, DmaQ):
            q.pre()
        tok = q.emit(fn())
        for b in reads:
            _merge(b.r, tok)
        for b in writes:
            b.w = dict(tok)
            b.r = {}
        return tok

    def dma_in(self, out, in_, reads=(), writes=()):
        return self.op(self.ld, lambda: self.nc.sync.dma_start(out=out, in_=in_), reads, writes)

    def dma_out(self, out, in_, reads=(), writes=()):
        tok = self.op(self.st, lambda: self.nc.gpsimd.dma_start(out=out, in_=in_), reads, writes)
        _merge(self.final, tok)
        return tok

    def mm(self, out, pairs, reads, wbuf, start=True, stop=True):
        deps = {}
        for b in reads:
            _merge(deps, b.w)
        _merge(deps, wbuf.w)
        _merge(deps, wbuf.r)
        self.pe.wait(deps)
        n = len(pairs)
        ins = None
        for i, (l, r) in enumerate(pairs):
            ins = self.nc.tensor.matmul(out, l, r, start=(start and i == 0), stop=(stop and i == n - 1))
        tok = self.pe.emit(ins)
        for b in reads:
            _merge(b.r, tok)
        wbuf.w = dict(tok)
        wbuf.r = {}
        return tok

    def barrier(self):
        toks = {}
        for q in (self.pe, self.act, self.dve, self.pool):
            if q.cnt > 0:
                toks[id(q.sem)] = (q.sem, q.cnt)
        for dq in (self.ld, self.st):
            for j in range(dq.NS):
                if dq.cnts[j] > 0:
                    toks[id(dq.sems[j])] = (dq.sems[j], 16 * dq.cnts[j])
        for q in (self.pe, self.act, self.dve, self.pool, self.ld):
            q.wait(toks)

    def finish(self):
        self.pool.wait(self.final)
        allq = {}
        for q in (self.pe, self.act, self.dve, self.pool):
            if q.cnt > 0:
                allq[id(q.sem)] = (q.sem, q.cnt)
        self.pool.wait(allq)


class Cfg:
    def __init__(self, N=16384, last=False, lam_init=0.2, debug=()):
        self.N = N
        self.NO = N // 4
        self.NOT = N - self.NO
        self.NA = NCTX + N + 256
        self.OWN0 = NCTX
        self.OTH0 = NCTX + self.NO
        self.HALO0 = NCTX + N
        self.NOC = NCTX + self.NO
        self.NKV = NCTX + N
        self.last = last
        self.lam_init = lam_init
        self.debug = debug
        self.TT = 512
        t = [(0, NCTX, "ctx")]
        for i in range(self.NO // 512):
            t.append((self.OWN0 + 512 * i, 512, "own"))
        for i in range(self.NOT // 512):
            t.append((self.OTH0 + 512 * i, 512, "oth"))
        t.append((self.HALO0, 256, "halo"))
        self.tiles = t
        self.qtiles = [x for x in t if x[2] in ("ctx", "own")]


IN_SPLITS = (256, 256, 512, 512, 16, 512, 512, 512, 512, 128, 128, 512, 256, 256, 8192)
IN_OFF = [0]
for _s in IN_SPLITS:
    IN_OFF.append(IN_OFF[-1] + _s)
D_IN = IN_OFF[-1]


def _seg(gi, lo=0, hi=None):
    w = IN_SPLITS[gi] if hi is None else hi
    return (IN_OFF[gi] + lo, IN_OFF[gi] + w)


def _blocks_from_segs(segs):
    blocks = []
    cur = []
    off = 0
    for (c0, c1) in segs:
        while c0 < c1:
            take = min(c1 - c0, 512 - off)
            cur.append((c0, c0 + take, off))
            off += take
            c0 += take
            if off == 512:
                blocks.append(cur)
                cur = []
                off = 0
    if cur:
        blocks.append(cur)
    return blocks


WA_FM_SEGS = [_seg(6), _seg(12), _seg(9)]
WA_TM_SEGS = [_seg(1), _seg(13), _seg(2), _seg(7), _seg(10), _seg(4)]
WB_SEGS = [_seg(0), _seg(1), _seg(3), _seg(5), _seg(8), _seg(11)]
WG_SEGS = [_seg(14)]


class Prog:
    def __init__(self, cfg):
        self.cfg = cfg
        self.nc = bass.Bass("TRN2", target_bir_lowering=False)
        self.K = Ker(self.nc)
        self.dram = {}
        self.dbuf = {}

    def din(self, name, shape, dt=F32):
        self.dram[name] = self.nc.dram_tensor(name, list(shape), dt, kind="ExternalInput").ap()
        self.dbuf[name] = Buf(name)
        return self.dram[name]

    def dout(self, name, shape, dt=F32):
        self.dram[name] = self.nc.dram_tensor(name, list(shape), dt, kind="ExternalOutput").ap()
        self.dbuf[name] = Buf(name)
        return self.dram[name]

    def dscr(self, name, shape, dt=BF16):
        if name in self.cfg.debug:
            return self.dout(name, shape, dt)
        self.dram[name] = self.nc.dram_tensor(name, list(shape), dt).ap()
        self.dbuf[name] = Buf(name)
        return self.dram[name]

    def dump(self, name, ap, shape, buf, dt=F32):
        d = self.dout(name, shape, dt)
        self.K.dma_out(d, ap, reads=[buf], writes=[self.dbuf[name]])

    def sb(self, es, name, shape, dt):
        return es.enter_context(self.nc.sbuf_tensor("s_" + name, list(shape), dt))

    def declare(self):
        c = self.cfg
        self.din("xT", [D, c.NA])
        self.din("cvec", [128, KC, 2])
        self.din("ada_w", [D, 6 * D])
        self.din("ada_b", [128, 96, 2])
        self.din("n1g", [128, KC, 2])
        self.din("n2g", [128, KC, 2])
        self.din("fng", [128, KC])
        self.din("w_in", [D, D_IN])
        self.din("gate_b", [128, 16])
        self.din("mng", [128, 4])
        self.din("dlam", [128, 4, 4, 64])
        self.din("dng", [128, 4])
        self.din("sink", [128, 8])
        self.din("gqg", [128, 1])
        self.din("gkg", [128, 1])
        self.din("w_branch", [4, 512, D])
        self.din("w_out", [D, D])
        self.din("w_up", [D, 2 * FFN])
        self.din("w_down", [FFN, D])
        self.din("cos64", [128, c.NA])
        self.din("sin64", [128, c.NA])
        self.din("cos128", [128, c.NA])
        self.din("sin128", [128, c.NA])
        self.din("rmats", [128, 2, 128])
        self.din("ident", [128, 128])
        self.din("trimask", [128, 2, 128])
        self.din("swamask", [128, 8, 512])
        self.din("mflags", [128, 2, c.NOT // 128])
        self.dout("outT", [D, c.NO])
        if not c.last:
            self.dout("ctxoT", [D, NCTX])
        nblk = lambda segs: len(_blocks_from_segs(segs))
        self.dscr("wA_fm", [nblk(WA_FM_SEGS), 1, 128, KC * 512])
        self.dscr("wA_tm", [nblk(WA_TM_SEGS), 1, 128, KC * 512])
        self.dscr("wB", [nblk(WB_SEGS), 1, 128, KC * 512])
        self.dscr("wG", [16, 1, 128, KC * 512])
        self.dscr("wBr", [16, 1, 128, 4 * 512])
        self.dscr("wO", [4, 1, 128, KC * 512])
        self.dscr("wU", [22, 1, 128, KC * 512])
        self.dscr("wD", [4, 4, 128, 11 * 512])
        self.dscr("hT", [D, c.NOC])
        self.dscr("dkT", [512, c.NA])
        self.dscr("gkT", [256, c.NA])
        self.dscr("skT", [128, c.NA])
        self.dscr("kvTM", [c.NA, 1664])
        self.dscr("gatesTM", [c.NA, 16], F32)
        self.dscr("mqT", [256, c.NOC])
        self.dscr("mkT", [256, c.NOC])
        self.dscr("soT", [512, c.NOC])
        self.dscr("dqT", [512, c.NOC])
        self.dscr("sqT", [512, c.NOC])
        self.dscr("gqT", [512, c.NOC])
        self.dscr("yT", [4, 512, c.NOC])

    def consts(self, es):
        K, nc = self.K, self.nc
        self.cb = Buf("consts")
        self.ones_bf = self.sb(es, "ones_bf", [128, 128], BF16)
        K.op(K.dve, lambda: nc.vector.memset(self.ones_bf[:], 1.0), writes=[self.cb])
        self.eps_t = self.sb(es, "eps_t", [128, 1], F32)
        K.op(K.dve, lambda: nc.vector.memset(self.eps_t[:], EPS), writes=[self.cb])
        self.ones_f = self.sb(es, "ones_f", [128, 128], F32)
        K.op(K.dve, lambda: nc.vector.memset(self.ones_f[:], 1.0), writes=[self.cb])
        tmp = self.sb(es, "c_tmp", [128, 2, 128], F32)
        K.dma_in(tmp[:], self.dram["rmats"], writes=[self.cb])
        self.rm = self.sb(es, "rm_bf", [128, 2, 128], BF16)
        K.op(K.dve, lambda: nc.vector.tensor_copy(self.rm[:], tmp[:]), reads=[self.cb], writes=[self.cb])
        self.ident_f = self.sb(es, "ident_f", [128, 128], F32)
        K.dma_in(self.ident_f[:], self.dram["ident"], writes=[self.cb])
        self.ident_bf = self.sb(es, "ident_bf", [128, 128], BF16)
        K.op(K.dve, lambda: nc.vector.tensor_copy(self.ident_bf[:], self.ident_f[:]), reads=[self.cb], writes=[self.cb])
        self.gqg = self.sb(es, "gqg", [128, 1], F32)
        K.dma_in(self.gqg[:], self.dram["gqg"], writes=[self.cb])
        self.gkg = self.sb(es, "gkg", [128, 1], F32)
        K.dma_in(self.gkg[:], self.dram["gkg"], writes=[self.cb])
        self.A1 = self.sb(es, "A1", [128, KC, 2], F32)
        self.B1 = self.sb(es, "B1", [128, KC, 2], F32)
        self.G1 = self.sb(es, "G1", [128, KC, 2], F32)
        self.A2 = self.sb(es, "A2", [128, KC, 2], F32)
        self.B2 = self.sb(es, "B2", [128, KC, 2], F32)
        self.G2 = self.sb(es, "G2", [128, KC, 2], F32)

    def cast_weights(self):
        K, nc, c = self.K, self.nc, self.cfg
        self.K.barrier()
        with ExitStack() as es:
            stf = RPool(K, es, "cst_f", 2, [128, KC * 512], F32)
            stb = RPool(K, es, "cst_b", 2, [128, KC * 512], BF16)
            cnt = [0]

            def do_block(dst_name, blk, kg, srcs, kpg):
                f, fb = stf.get()
                b, bb = stb.get()
                f3 = f[:, 0:kpg * 512].rearrange("p (k c) -> p k c", c=512)
                b3 = b[:, 0:kpg * 512].rearrange("p (k c) -> p k c", c=512)
                wtot = 0
                for (src, off, w) in srcs:
                    K.dma_in(f3[:, :, off:off + w], src, writes=[fb])
                    wtot = max(wtot, off + w)
                eng = [K.dve, K.pool, K.act][cnt[0] % 3]
                cnt[0] += 1
                if eng is K.act:
                    K.op(eng, lambda: nc.scalar.copy(b3[:, :, 0:wtot], f3[:, :, 0:wtot]), reads=[fb], writes=[bb])
                elif eng is K.dve:
                    K.op(eng, lambda: nc.vector.tensor_copy(b3[:, :, 0:wtot], f3[:, :, 0:wtot]), reads=[fb], writes=[bb])
                else:
                    K.op(eng, lambda: nc.gpsimd.tensor_copy(b3[:, :, 0:wtot], f3[:, :, 0:wtot]), reads=[fb], writes=[bb])
                K.dma_out(self.dram[dst_name][blk, kg, :, 0:kpg * 512].rearrange("p (k c) -> p k c", c=512)[:, :, 0:wtot],
                          b3[:, :, 0:wtot], reads=[bb], writes=[self.dbuf[dst_name]])

            win3 = self.dram["w_in"].rearrange("(k p) c -> p k c", p=128)
            for name, segs in (("wA_fm", WA_FM_SEGS), ("wA_tm", WA_TM_SEGS), ("wB", WB_SEGS), ("wG", WG_SEGS)):
                for blk, pieces in enumerate(_blocks_from_segs(segs)):
                    do_block(name, blk, 0, [(win3[:, :, c0:c1], off, c1 - c0) for (c0, c1, off) in pieces], KC)
            for br in range(4):
                wb3 = self.dram["w_branch"][br].rearrange("(k p) c -> p k c", p=128)
                for j in range(4):
                    do_block("wBr", br * 4 + j, 0, [(wb3[:, :, j * 512:(j + 1) * 512], 0, 512)], 4)
            wo3 = self.dram["w_out"].rearrange("(k p) c -> p k c", p=128)
            for j in range(4):
                do_block("wO", j, 0, [(wo3[:, :, j * 512:(j + 1) * 512], 0, 512)], KC)
            wu3 = self.dram["w_up"].rearrange("(k p) c -> p k c", p=128)
            for j in range(22):
                do_block("wU", j, 0, [(wu3[:, :, j * 512:(j + 1) * 512], 0, 512)], KC)
            wd3 = self.dram["w_down"].rearrange("(k p) c -> p k c", p=128)
            for j in range(4):
                for kg in range(4):
                    do_block("wD", j, kg, [(wd3[:, kg * 11:(kg + 1) * 11, j * 512:(j + 1) * 512], 0, 512)], 11)

    def phase_mod(self):
        K, nc = self.K, self.nc
        self.K.barrier()
        with ExitStack() as es:
            cv = self.sb(es, "m_cv", [128, KC, 2], F32)
            sv = self.sb(es, "m_sv", [128, KC, 2], F32)
            ab = self.sb(es, "m_ab", [128, 96, 2], F32)
            mod = self.sb(es, "m_mod", [128, 96, 2], F32)
            g1 = self.sb(es, "m_g1", [128, KC, 2], F32)
            g2 = self.sb(es, "m_g2", [128, KC, 2], F32)
            b = Buf("modsmall")
            K.dma_in(cv[:], self.dram["cvec"], writes=[b])
            K.dma_in(ab[:], self.dram["ada_b"], writes=[b])
            K.dma_in(g1[:], self.dram["n1g"], writes=[b])
            K.dma_in(g2[:], self.dram["n2g"], writes=[b])
            K.op(K.act, lambda: nc.scalar.activation(out=sv[:], in_=cv[:], func=AF.Silu), reads=[b], writes=[b])
            wst = RPool(K, es, "m_w", 2, [128, KC, 512], F32)
            self.psum = RPool(K, es, "psm", 2, [128, 512], F32, psum=True)
            ps, pb = self.psum.get()
            aw3 = self.dram["ada_w"].rearrange("(k p) c -> p k c", p=128)
            for blk in range(24):
                w, wb = wst.get()
                K.dma_in(w[:], aw3[:, :, blk * 512:(blk + 1) * 512], writes=[wb])
                for j in range(4):
                    fc = blk * 4 + j
                    K.mm(ps[:, fc * 2:fc * 2 + 2],
                         [(w[:, kc, j * 128:(j + 1) * 128], sv[:, kc, :]) for kc in range(KC)],
                         reads=[wb, b], wbuf=pb)
            mb = Buf("mod")
            K.op(K.dve, lambda: nc.vector.tensor_tensor(out=mod[:].rearrange("p a b -> p (a b)"), in0=ps[:, 0:192],
                                                        in1=ab[:].rearrange("p a b -> p (a b)"), op=ALU.add),
                 reads=[pb, b], writes=[mb])
            cbuf = self.cb

            def mk_a(dst, gain, sc0):
                K.op(K.dve, lambda: nc.vector.scalar_tensor_tensor(out=dst[:], in0=mod[:, sc0:sc0 + 16, :], scalar=1.0,
                                                                   in1=gain[:], op0=ALU.add, op1=ALU.mult),
                     reads=[mb, b], writes=[cbuf])

            def cp(dst, s0):
                K.op(K.dve, lambda: nc.vector.tensor_copy(dst[:], mod[:, s0:s0 + 16, :]), reads=[mb], writes=[cbuf])

            cp(self.B1, 0)
            mk_a(self.A1, g1, 16)
            cp(self.G1, 32)
            cp(self.B2, 48)
            mk_a(self.A2, g2, 64)
            cp(self.G2, 80)

    def ep_store(self, src_ap, src_buf, dname, r0, c0, TT):
        self.K.dma_out(self.dram[dname][r0:r0 + 128, c0:c0 + TT], src_ap, reads=[src_buf], writes=[self.dbuf[dname]])

    def ep_copy(self, ps, pb, TT, dname, r0, c0, scale=1.0, func=None):
        K, nc = self.K, self.nc
        y, yb = self.ybp.get()
        f = AF.Copy if func is None else func
        K.op(K.act, lambda: nc.scalar.activation(out=y[:, :TT], in_=ps[:, :TT], func=f, scale=scale), reads=[pb], writes=[yb])
        self.ep_store(y[:, :TT], yb, dname, r0, c0, TT)

    def rope_from(self, y, yb, TT, rt, rtb, which, dname, r0, c0):
        K, nc = self.K, self.nc
        ci = 0 if which == 64 else 2
        ridx = 0 if which == 64 else 1
        rp, rpb = self.psum.get()
        K.mm(rp[:, :TT], [(self.rm[:, ridx, :], y[:, :TT])], reads=[yb, self.cb], wbuf=rpb)
        t1, t1b = self.tfp.get()
        K.op(K.dve, lambda: nc.vector.tensor_tensor(out=t1[:, :TT], in0=y[:, :TT], in1=rt[:, ci, :TT], op=ALU.mult),
             reads=[yb, rtb], writes=[t1b])
        t2, t2b = self.tfp.get()
        K.op(K.dve, lambda: nc.vector.tensor_tensor(out=t2[:, :TT], in0=rp[:, :TT], in1=rt[:, ci + 1, :TT], op=ALU.mult),
             reads=[rpb, rtb], writes=[t2b])
        o, ob = self.ybp.get()
        K.op(K.pool, lambda: nc.gpsimd.tensor_tensor(out=o[:, :TT], in0=t1[:, :TT], in1=t2[:, :TT], op=ALU.add),
             reads=[t1b, t2b], writes=[ob])
        self.ep_store(o[:, :TT], ob, dname, r0, c0, TT)

    def ep_rope(self, ps, pb, TT, rt, rtb, which, dname, r0, c0):
        K, nc = self.K, self.nc
        y, yb = self.ybp.get()
        K.op(K.act, lambda: nc.scalar.copy(y[:, :TT], ps[:, :TT]), reads=[pb], writes=[yb])
        self.rope_from(y, yb, TT, rt, rtb, which, dname, r0, c0)

    def rstd_from_ps(self, ssp, sspb, TT, n):
        K, nc = self.K, self.nc
        r1, r1b = self.rsp.get()
        K.op(K.act, lambda: nc.scalar.activation(out=r1[:, :TT], in_=ssp[:, :TT], func=AF.Sqrt, bias=self.eps_t[:, 0:1],
                                                 scale=1.0 / n), reads=[sspb, self.cb], writes=[r1b])
        K.op(K.dve, lambda: nc.vector.reciprocal(out=r1[:, :TT], in_=r1[:, :TT]), reads=[r1b], writes=[r1b])
        return r1, r1b

    def ep_norm_rope(self, ps, pb, TT, gain, rt, rtb, dname, r0, c0):
        K, nc = self.K, self.nc
        sq, sqb = self.ybp.get()
        K.op(K.act, lambda: nc.scalar.activation(out=sq[:, :TT], in_=ps[:, :TT], func=AF.Square), reads=[pb], writes=[sqb])
        ssp, sspb = self.psum.get()
        K.mm(ssp[:, :TT], [(self.ones_bf[:], sq[:, :TT])], reads=[sqb, self.cb], wbuf=sspb)
        r1, r1b = self.rstd_from_ps(ssp, sspb, TT, 128.0)
        y, yb = self.ybp.get()
        K.op(K.dve, lambda: nc.vector.scalar_tensor_tensor(out=y[:, :TT], in0=ps[:, :TT], scalar=gain[:, 0:1],
                                                           in1=r1[:, :TT], op0=ALU.mult, op1=ALU.mult),
             reads=[pb, r1b, self.cb], writes=[yb])
        self.rope_from(y, yb, TT, rt, rtb, 128, dname, r0, c0)

    def norm_mod(self, es_tile, xt, xtb, sqt, sqtb, hT, hTb, TT, A, Bv, v):
        K, nc = self.K, self.nc
        K.op(K.act, lambda: nc.scalar.activation(out=sqt[:, :, :TT], in_=xt[:, :, :TT], func=AF.Square),
             reads=[xtb], writes=[sqtb])
        ssp, sspb = self.psum.get()
        K.mm(ssp[:, :TT], [(self.ones_bf[:], sqt[:, kc, :TT]) for kc in range(KC)], reads=[sqtb, self.cb], wbuf=sspb)
        r1, r1b = self.rstd_from_ps(ssp, sspb, TT, float(D))
        for kc in range(KC):
            t, tb = self.tfp.get()
            K.op(K.dve, lambda: nc.vector.scalar_tensor_tensor(out=t[:, :TT], in0=xt[:, kc, :TT], scalar=A[:, kc, v:v + 1],
                                                               in1=r1[:, :TT], op0=ALU.mult, op1=ALU.mult),
                 reads=[xtb, r1b, self.cb], writes=[tb])
            K.op(K.act, lambda: nc.scalar.activation(out=hT[:, kc, :TT], in_=t[:, :TT], func=AF.Identity,
                                                     bias=Bv[:, kc, v:v + 1], scale=1.0),
                 reads=[tb, self.cb], writes=[hTb])

    def phase1a(self):
        K, nc, c = self.K, self.nc, self.cfg
        self.K.barrier()
        with ExitStack() as es:
            self.psum = RPool(K, es, "psa", 8, [128, 512], F32, psum=True)
            self.ybp = RPool(K, es, "a_yb", 6, [128, 512], BF16)
            self.tfp = RPool(K, es, "a_tf", 4, [128, 512], F32)
            self.rsp = RPool(K, es, "a_rs", 2, [128, 512], F32)
            xt = self.sb(es, "a_xt", [128, KC, 512], F32); xtb = Buf()
            sqt = self.sb(es, "a_sq", [128, KC, 512], BF16); sqtb = Buf()
            hT = self.sb(es, "a_hT", [128, KC, 512], BF16); hTb = Buf()
            rt = self.sb(es, "a_rt", [128, 4, 512], F32); rtb = Buf()
            wfm0 = self.sb(es, "a_wfm0", [128, KC, 512], BF16)
            wfm1 = self.sb(es, "a_wfm1", [128, KC, 384], BF16)
            wtm = [self.sb(es, f"a_wtm{i}", [128, KC, 512], BF16) for i in range(3)]
            wtm3 = self.sb(es, "a_wtm3", [128, KC, 144], BF16)
            gb = self.sb(es, "a_gb", [128, 16], F32)
            wb = Buf("wA")
            r3 = lambda name, blk: self.dram[name][blk, 0].rearrange("p (k c) -> p k c", c=512)
            K.dma_in(wfm0[:], r3("wA_fm", 0), reads=[self.dbuf["wA_fm"]], writes=[wb])
            K.dma_in(wfm1[:], r3("wA_fm", 1)[:, :, 0:384], reads=[self.dbuf["wA_fm"]], writes=[wb])
            for i in range(3):
                K.dma_in(wtm[i][:], r3("wA_tm", i), reads=[self.dbuf["wA_tm"]], writes=[wb])
            K.dma_in(wtm3[:], r3("wA_tm", 3)[:, :, 0:144], reads=[self.dbuf["wA_tm"]], writes=[wb])
            K.dma_in(gb[:], self.dram["gate_b"], writes=[wb])
            stg = RPool(K, es, "a_stg", 2, [128, 1664], BF16)
            gst = RPool(K, es, "a_gst", 2, [128, 16], F32)
            x3 = self.dram["xT"].rearrange("(k p) a -> p k a", p=128)
            h3 = self.dram["hT"].rearrange("(k p) a -> p k a", p=128)
            for (a0, TT, kind) in c.tiles:
                v = 1 if kind == "ctx" else 0
                K.dma_in(xt[:, :, :TT], x3[:, :, a0:a0 + TT], writes=[xtb])
                for i, nm in enumerate(("cos64", "sin64", "cos128", "sin128")):
                    K.dma_in(rt[:, i, :TT], self.dram[nm][:, a0:a0 + TT], writes=[rtb])
                self.norm_mod(es, xt, xtb, sqt, sqtb, hT, hTb, TT, self.A1, self.B1, v)
                if kind in ("ctx", "own"):
                    K.dma_out(h3[:, :, a0:a0 + TT], hT[:, :, :TT], reads=[hTb], writes=[self.dbuf["hT"]])
                for ch in range(7):
                    wt = wfm0 if ch < 4 else wfm1
                    j = ch if ch < 4 else ch - 4
                    ps, pb = self.psum.get()
                    K.mm(ps[:, :TT], [(wt[:, kc, j * 128:(j + 1) * 128], hT[:, kc, :TT]) for kc in range(KC)],
                         reads=[wb, hTb], wbuf=pb)
                    if ch < 4:
                        self.ep_rope(ps, pb, TT, rt, rtb, 64, "dkT", ch * 128, a0)
                    elif ch < 6:
                        self.ep_norm_rope(ps, pb, TT, self.gkg, rt, rtb, "gkT", (ch - 4) * 128, a0)
                    else:
                        self.ep_rope(ps, pb, TT, rt, rtb, 64, "skT", 0, a0)
                for sub in range(TT // 128):
                    s, sbf = stg.get()
                    g, gbf = gst.get()
                    pss = []
                    for gi in range(4):
                        w = 512 if gi < 3 else 144
                        wt = wtm[gi] if gi < 3 else wtm3
                        ps, pb = self.psum.get()
                        K.mm(ps[:, :w], [(hT[:, kc, sub * 128:(sub + 1) * 128], wt[:, kc, 0:w]) for kc in range(KC)],
                             reads=[wb, hTb], wbuf=pb)
                        pss.append((ps, pb))
                    (p0, b0), (p1, b1), (p2, b2), (p3, b3) = pss
                    K.op(K.act, lambda: nc.scalar.activation(out=s[:, 0:256], in_=p0[:, 0:256], func=AF.Copy, scale=0.125),
                         reads=[b0], writes=[sbf])
                    K.op(K.dve, lambda: nc.vector.tensor_copy(s[:, 256:512], p0[:, 256:512]), reads=[b0], writes=[sbf])
                    K.op(K.dve, lambda: nc.vector.tensor_copy(s[:, 512:1024], p1[:, 0:512]), reads=[b1], writes=[sbf])
                    K.op(K.act, lambda: nc.scalar.copy(s[:, 1024:1536], p2[:, 0:512]), reads=[b2], writes=[sbf])
                    K.op(K.dve, lambda: nc.vector.tensor_copy(s[:, 1536:1664], p3[:, 0:128]), reads=[b3], writes=[sbf])
                    K.op(K.dve, lambda: nc.vector.tensor_tensor(out=g[:], in0=p3[:, 128:144], in1=gb[:], op=ALU.add),
                         reads=[b3, wb], writes=[gbf])
                    r0 = a0 + sub * 128
                    K.dma_out(self.dram["kvTM"][r0:r0 + 128, :], s[:], reads=[sbf], writes=[self.dbuf["kvTM"]])
                    K.dma_out(self.dram["gatesTM"][r0:r0 + 128, :], g[:], reads=[gbf], writes=[self.dbuf["gatesTM"]])

    def fm_gemm(self, act, act_bufs, TT, wname, blocks, nkg, kpg, cb, wpool, nch=None):
        K = self.K
        for blk in blocks:
            n = 4 if nch is None else nch(blk)
            pss = [self.psum.get() for _ in range(n)]
            for kg in range(nkg):
                wt, wtb = wpool.get()
                K.dma_in(wt[:, 0:kpg * 512], self.dram[wname][blk, kg], reads=[self.dbuf[wname]], writes=[wtb])
                for j in range(n):
                    ps, pb = pss[j]
                    K.mm(ps[:, :TT], [(wt[:, k * 512 + j * 128:k * 512 + (j + 1) * 128], act(kg * kpg + k)) for k in range(kpg)],
                         reads=[wtb] + list(act_bufs), wbuf=pb, start=(kg == 0), stop=(kg == nkg - 1))
            for j in range(n):
                cb(blk, j, pss[j][0], pss[j][1])

    def phase1b(self):
        K, nc, c = self.K, self.nc, self.cfg
        self.K.barrier()
        with ExitStack() as es:
            self.psum = RPool(K, es, "psb", 8, [128, 512], F32, psum=True)
            self.ybp = RPool(K, es, "b_yb", 6, [128, 512], BF16)
            self.tfp = RPool(K, es, "b_tf", 4, [128, 512], F32)
            self.rsp = RPool(K, es, "b_rs", 2, [128, 512], F32)
            hp = RPool(K, es, "b_hT", 2, [128, KC, 512], BF16)
            rtp = RPool(K, es, "b_rt", 2, [128, 4, 512], F32)
            wpool = RPool(K, es, "b_w", 2, [128, KC * 512], BF16)
            h3 = self.dram["hT"].rearrange("(k p) a -> p k a", p=128)
            for (a0, TT, kind) in c.qtiles:
                hT, hTb = hp.get()
                rt, rtb = rtp.get()
                K.dma_in(hT[:, :, :TT], h3[:, :, a0:a0 + TT], reads=[self.dbuf["hT"]], writes=[hTb])
                for i, nm in enumerate(("cos64", "sin64", "cos128", "sin128")):
                    K.dma_in(rt[:, i, :TT], self.dram[nm][:, a0:a0 + TT], writes=[rtb])

                def cb(blk, j, ps, pb, TT=TT, a0=a0, rt=rt, rtb=rtb):
                    ch = blk * 4 + j
                    if ch < 2:
                        self.ep_copy(ps, pb, TT, "mqT", ch * 128, a0)
                    elif ch < 4:
                        self.ep_copy(ps, pb, TT, "mkT", (ch - 2) * 128, a0, scale=0.125)
                    elif ch < 8:
                        self.ep_copy(ps, pb, TT, "soT", (ch - 4) * 128, a0, func=AF.Sigmoid)
                    elif ch < 12:
                        self.ep_rope(ps, pb, TT, rt, rtb, 64, "dqT", (ch - 8) * 128, a0)
                    elif ch < 16:
                        self.ep_rope(ps, pb, TT, rt, rtb, 64, "sqT", (ch - 12) * 128, a0)
                    else:
                        self.ep_norm_rope(ps, pb, TT, self.gqg, rt, rtb, "gqT", (ch - 16) * 128, a0)

                self.fm_gemm(lambda k, hT=hT, TT=TT: hT[:, k, :TT], [hTb], TT, "wB", range(5), 1, KC, cb, wpool)


def _pk(v):
    return np.ascontiguousarray(np.asarray(v, np.float32).reshape(-1, 128).T)


def rope_tables(pos, valid, d):
    nf = d // 4
    inv = (np.float32(10000.0) ** (-(np.arange(nf, dtype=np.float32) / np.float32(nf)))).astype(np.float32)
    row = (pos // GRID_W).astype(np.float32)
    col = (pos % GRID_W).astype(np.float32)
    p = np.arange(128)
    j = p % d
    axis = j // (d // 2)
    f = j % nf
    ang = np.where(axis[:, None] == 0, row[None, :], col[None, :]).astype(np.float32) * inv[f][:, None]
    cos = np.cos(ang).astype(np.float32)
    sin = np.sin(ang).astype(np.float32)
    cos[:, ~valid] = 1.0
    sin[:, ~valid] = 0.0
    return np.ascontiguousarray(cos), np.ascontiguousarray(sin)


def rot_mats():
    R = np.zeros((128, 2, 128), np.float32)
    for idx, d in enumerate((64, 128)):
        q = d // 4
        for pp in range(128):
            j = pp % d
            half = (j % (d // 2)) // q
            if half == 0:
                R[pp + q, idx, pp] = -1.0
            else:
                R[pp - q, idx, pp] = 1.0
    return R


def swa_masks(r, nr):
    m = np.zeros((128, 8, 512), np.float32)
    kl = np.arange(128)[:, None]
    ql = np.arange(512)[None, :]
    for i, o in enumerate(range(-1, 5)):
        m[:, i, :] = (np.abs(128 * o + kl - ql) <= 128)
    m[:, 6, :] = m[:, 0, :] * (1.0 if r > 0 else 0.0)
    m[:, 7, :] = m[:, 5, :] * (1.0 if r < nr - 1 else 0.0)
    return m


def prep_core_inputs(inputs, l, x_in, ctx_in, b, r, cfg):
    N, NO = cfg.N, cfg.NO
    o0, o1 = r * NO, (r + 1) * NO
    xb = x_in[b]
    halo = np.zeros((256, D), np.float32)
    if o0 >= 128:
        halo[0:128] = xb[o0 - 128:o0]
    if o1 + 128 <= N:
        halo[128:256] = xb[o1:o1 + 128]
    xa = np.concatenate([ctx_in[b], xb[o0:o1], xb[:o0], xb[o1:], halo], axis=0)
    xT = np.ascontiguousarray(xa.T)
    pos = np.concatenate([np.zeros(NCTX, np.int64), np.arange(o0, o1), np.arange(0, o0), np.arange(o1, N),
                          np.arange(o0 - 128, o0), np.arange(o1, o1 + 128)])
    valid = np.ones(cfg.NA, bool)
    valid[:NCTX] = False
    valid[cfg.HALO0:] = (pos[cfg.HALO0:] >= 0) & (pos[cfg.HALO0:] < N)
    pos = np.clip(pos, 0, N - 1)
    c64, s64 = rope_tables(pos, valid, 64)
    c128, s128 = rope_tables(pos, valid, 128)
    f32 = np.float32
    cvec = np.stack([_pk(inputs["c"][b]), _pk(inputs["c_ctx"])], axis=-1)
    ab = _pk(inputs["ada_b"][l])
    nch = cfg.NOT // 128
    fl = (np.arange(nch) < (o0 // 128)).astype(f32)
    mflags = np.broadcast_to(np.stack([fl, 1.0 - fl], 0)[None], (128, 2, nch))
    rep = lambda a, n: np.ascontiguousarray(np.broadcast_to(np.asarray(a, f32).reshape(1, -1), (128, n)))
    m = {
        "xT": xT,
        "cvec": np.ascontiguousarray(cvec),
        "ada_w": np.ascontiguousarray(inputs["ada_w"][l]),
        "ada_b": np.ascontiguousarray(np.stack([ab, ab], -1)),
        "n1g": np.ascontiguousarray(np.stack([_pk(inputs["norm1_g"][l])] * 2, -1)),
        "n2g": np.ascontiguousarray(np.stack([_pk(inputs["norm2_g"][l])] * 2, -1)),
        "fng": _pk(inputs["final_norm_g"]),
        "w_in": np.ascontiguousarray(inputs["w_in"][l]),
        "gate_b": rep(inputs["mlstm_gate_b"][l].reshape(-1), 16),
        "mng": np.ascontiguousarray(np.asarray(inputs["mlstm_norm_g"][l], f32).reshape(4, 128).T),
        "dlam": np.ascontiguousarray(np.broadcast_to(np.asarray(inputs["diff_lambda"][l], f32)[None], (128, 4, 4, 64))),
        "dng": np.ascontiguousarray(np.asarray(inputs["diff_norm_g"][l], f32).reshape(4, 128).T),
        "sink": rep(inputs["swa_sink"][l], 8),
        "gqg": np.ascontiguousarray(np.asarray(inputs["gqa_q_norm_g"][l], f32).reshape(128, 1)),
        "gkg": np.ascontiguousarray(np.asarray(inputs["gqa_k_norm_g"][l], f32).reshape(128, 1)),
        "w_branch": np.ascontiguousarray(inputs["w_branch"][l]),
        "w_out": np.ascontiguousarray(inputs["w_out"][l]),
        "w_up": np.ascontiguousarray(inputs["w_up"][l]),
        "w_down": np.ascontiguousarray(inputs["w_down"][l]),
        "cos64": c64, "sin64": s64, "cos128": c128, "sin128": s128,
        "rmats": rot_mats(),
        "ident": np.eye(128, dtype=f32),
        "trimask": np.ascontiguousarray(np.stack([np.triu(np.ones((128, 128), f32)), np.tril(np.ones((128, 128), f32))], 1)),
        "swamask": swa_masks(r, 4),
        "mflags": np.ascontiguousarray(mflags, dtype=f32),
    }
    return m


def _attn_methods():
    def phase_attn(self):
        K, nc, c = self.K, self.nc, self.cfg
        K.barrier()
        NKV = c.NKV
        nkb = NKV // 128
        with ExitStack() as es:
            self.psum = RPool(K, es, "pst", 3, [128, 1024], F32, psum=True)
            accs = RPool(K, es, "psacc", 2, [128, 512], F32, psum=True)
            (accO, accOb), (accL, accLb) = accs.tiles
            self.ybp = RPool(K, es, "t_yb", 4, [128, 512], BF16)
            self.tfp = RPool(K, es, "t_tf", 4, [128, 512], F32)
            self.rsp = RPool(K, es, "t_rs", 2, [128, 512], F32)
            pp = RPool(K, es, "t_p", 3, [128, 1024], BF16)
            qp = RPool(K, es, "t_q", 2, [128, 512], BF16)
            op_ = RPool(K, es, "t_o", 3, [128, 512], F32)
            kT = self.sb(es, "t_kT", [128, NKV], BF16); kTb = Buf()
            vv = self.sb(es, "t_vv", [128, nkb, 128], BF16); vvb = Buf()
            sm = self.sb(es, "t_sm", [128, 8, 512], BF16)
            smf = self.sb(es, "t_smf", [128, 8, 512], F32)
            small = Buf("attn_small")
            K.dma_in(smf[:], self.dram["swamask"], writes=[small])
            K.op(K.dve, lambda: nc.vector.tensor_copy(sm[:], smf[:]), reads=[small], writes=[small])
            dl = self.sb(es, "t_dl", [128, 4, 4, 64], F32)
            K.dma_in(dl[:], self.dram["dlam"], writes=[small])
            pr = self.sb(es, "t_pr", [128, 2, 4, 64], F32)
            K.op(K.dve, lambda: nc.vector.tensor_tensor(out=pr[:, 0], in0=dl[:, 0], in1=dl[:, 1], op=ALU.mult), reads=[small], writes=[small])
            K.op(K.dve, lambda: nc.vector.tensor_tensor(out=pr[:, 1], in0=dl[:, 2], in1=dl[:, 3], op=ALU.mult), reads=[small], writes=[small])
            sums = self.sb(es, "t_sums", [128, 8], F32)
            K.op(K.dve, lambda: nc.vector.reduce_sum(out=sums[:], in_=pr[:].rearrange("p a h d -> p (a h) d"),
                                                     axis=mybir.AxisListType.X), reads=[small], writes=[small])
            K.op(K.act, lambda: nc.scalar.activation(out=sums[:], in_=sums[:], func=AF.Exp), reads=[small], writes=[small])
            nlam = self.sb(es, "t_nlam", [128, 4], F32)
            K.op(K.dve, lambda: nc.vector.tensor_tensor(out=nlam[:], in0=sums[:, 4:8], in1=sums[:, 0:4], op=ALU.subtract),
                 reads=[small], writes=[small])
            K.op(K.dve, lambda: nc.vector.tensor_scalar(out=nlam[:], in0=nlam[:], scalar1=-float(c.lam_init), scalar2=None,
                                                        op0=ALU.add), reads=[small], writes=[small])
            dg = self.sb(es, "t_dg", [128, 4], F32)
            K.dma_in(dg[:], self.dram["dng"], writes=[small])
            K.op(K.dve, lambda: nc.vector.tensor_scalar(out=dg[:], in0=dg[:], scalar1=float(1.0 - c.lam_init), scalar2=None,
                                                        op0=ALU.mult), reads=[small], writes=[small])
            esk = self.sb(es, "t_esk", [128, 8], F32)
            K.dma_in(esk[:], self.dram["sink"], writes=[small])
            K.op(K.act, lambda: nc.scalar.activation(out=esk[:], in_=esk[:], func=AF.Exp), reads=[small], writes=[small])

            def unit(q, qb, p0, p1, TT, blocks, scale, dv):
                n = len(blocks)
                prs = [blocks[i:i + 2] for i in range(0, n, 2)]
                npr = len(prs)
                Ss = {}
                Ps = {}

                def qk(j):
                    S, Sb = self.psum.get()
                    for u, (k_ap, v_ap, m_ap, bufs) in enumerate(prs[j]):
                        K.mm(S[:, u * 512:u * 512 + TT], [(k_ap, q[p0:p1, :TT])], reads=[qb] + bufs, wbuf=Sb)
                    Ss[j] = (S, Sb)

                def ex(j):
                    S, Sb = Ss.pop(j)
                    P, Pb = pp.get()
                    nb = len(prs[j])
                    K.op(K.act, lambda: nc.scalar.activation(out=P[:].rearrange("p (u t) -> p u t", u=2)[:, 0:nb, 0:TT],
                                                             in_=S[:].rearrange("p (u t) -> p u t", u=2)[:, 0:nb, 0:TT],
                                                             func=AF.Exp, scale=scale), reads=[Sb], writes=[Pb])
                    for u, (k_ap, v_ap, m_ap, bufs) in enumerate(prs[j]):
                        if m_ap is not None:
                            K.op(K.pool, lambda: nc.gpsimd.tensor_tensor(out=P[:, u * 512:u * 512 + TT], in0=P[:, u * 512:u * 512 + TT],
                                                                         in1=m_ap[:, :TT], op=ALU.mult), reads=[Pb, small], writes=[Pb])
                    Ps[j] = (P, Pb)

                def pv(j):
                    P, Pb = Ps.pop(j)
                    for u, (k_ap, v_ap, m_ap, bufs) in enumerate(prs[j]):
                        i = 2 * j + u
                        K.mm(accO[0:dv, :TT], [(v_ap, P[:, u * 512:u * 512 + TT])], reads=[Pb] + bufs, wbuf=accOb,
                             start=(i == 0), stop=(i == n - 1))
                        K.mm(accL[:, :TT], [(self.ones_bf[:], P[:, u * 512:u * 512 + TT])], reads=[Pb, self.cb], wbuf=accLb,
                             start=(i == 0), stop=(i == n - 1))

                qk(0)
                if npr > 1:
                    qk(1)
                for j in range(npr):
                    ex(j)
                    pv(j)
                    if j + 2 < npr:
                        qk(j + 2)

            def recip_L(TT, add_ap=None):
                r, rb = self.rsp.get()
                if add_ap is None:
                    K.op(K.dve, lambda: nc.vector.reciprocal(out=r[:, :TT], in_=accL[:, :TT]), reads=[accLb], writes=[rb])
                else:
                    K.op(K.dve, lambda: nc.vector.tensor_scalar(out=r[:, :TT], in0=accL[:, :TT], scalar1=add_ap, scalar2=None,
                                                                op0=ALU.add), reads=[accLb, small], writes=[rb])
                    K.op(K.dve, lambda: nc.vector.reciprocal(out=r[:, :TT], in_=r[:, :TT]), reads=[rb], writes=[rb])
                return r, rb

            def load_q(name, row0, a0, TT):
                q, qb = qp.get()
                K.dma_in(q[:, :TT], self.dram[name][row0:row0 + 128, a0:a0 + TT], reads=[self.dbuf[name]], writes=[qb])
                return q, qb

            def store_y(src, srcb, br, row0, nrows, a0, TT):
                K.dma_out(self.dram["yT"][br, row0:row0 + nrows, a0:a0 + TT], src, reads=[srcb], writes=[self.dbuf["yT"]])

            qtiles = [t for t in c.qtiles if not (c.last and t[2] == "ctx")]
            kv3 = self.dram["kvTM"][0:NKV, :].rearrange("(b p) c -> p b c", p=128)

            for h in range(4):
                K.dma_in(kT[:], self.dram["dkT"][h * 128:(h + 1) * 128, 0:NKV], reads=[self.dbuf["dkT"]], writes=[kTb])
                for b0 in range(0, nkb, 16):
                    b1 = min(nkb, b0 + 16)
                    K.dma_in(vv[:, b0:b1, :], kv3[:, b0:b1, 1024 + h * 128:1024 + (h + 1) * 128], reads=[self.dbuf["kvTM"]], writes=[vvb])
                for (a0, TT, kind) in qtiles:
                    q, qb = load_q("dqT", h * 128, a0, TT)
                    kbs = range(2) if kind == "ctx" else range(nkb)
                    os_ = []
                    for m in range(2):
                        p0, p1 = 64 * m, 64 * m + 64
                        blocks = [(kT[p0:p1, kb * 128:(kb + 1) * 128], vv[:, kb, :], None, [kTb, vvb]) for kb in kbs]
                        unit(q, qb, p0, p1, TT, blocks, 0.125, 128)
                        r, rb = recip_L(TT)
                        o, ob = op_.get()
                        K.op(K.dve, lambda: nc.vector.tensor_tensor(out=o[:, :TT], in0=accO[:, :TT], in1=r[:, :TT], op=ALU.mult),
                             reads=[accOb, rb], writes=[ob])
                        os_.append((o, ob))
                    (o1, o1b), (o2, o2b) = os_
                    cm, cmb = op_.get()
                    K.op(K.dve, lambda: nc.vector.scalar_tensor_tensor(out=cm[:, :TT], in0=o2[:, :TT], scalar=nlam[:, h:h + 1],
                                                                       in1=o1[:, :TT], op0=ALU.mult, op1=ALU.add),
                         reads=[o1b, o2b, small], writes=[cmb])
                    sq, sqb = self.ybp.get()
                    K.op(K.act, lambda: nc.scalar.activation(out=sq[:, :TT], in_=cm[:, :TT], func=AF.Square), reads=[cmb], writes=[sqb])
                    ssp, sspb = self.psum.get()
                    K.mm(ssp[:, :TT], [(self.ones_bf[:], sq[:, :TT])], reads=[sqb, self.cb], wbuf=sspb)
                    r1, r1b = self.rstd_from_ps(ssp, sspb, TT, 128.0)
                    y, yb = self.ybp.get()
                    K.op(K.dve, lambda: nc.vector.scalar_tensor_tensor(out=y[:, :TT], in0=cm[:, :TT], scalar=dg[:, h:h + 1],
                                                                       in1=r1[:, :TT], op0=ALU.mult, op1=ALU.mult),
                         reads=[cmb, r1b, small], writes=[yb])
                    store_y(y[:, :TT], yb, 1, h * 128, 128, a0, TT)

            for g in range(2):
                K.dma_in(kT[:], self.dram["gkT"][g * 128:(g + 1) * 128, 0:NKV], reads=[self.dbuf["gkT"]], writes=[kTb])
                for b0 in range(0, nkb, 16):
                    b1 = min(nkb, b0 + 16)
                    K.dma_in(vv[:, b0:b1, :], kv3[:, b0:b1, 256 + g * 128:256 + (g + 1) * 128], reads=[self.dbuf["kvTM"]], writes=[vvb])
                for h in (2 * g, 2 * g + 1):
                    for (a0, TT, kind) in qtiles:
                        q, qb = load_q("gqT", h * 128, a0, TT)
                        kbs = range(2) if kind == "ctx" else range(nkb)
                        blocks = [(kT[:, kb * 128:(kb + 1) * 128], vv[:, kb, :], None, [kTb, vvb]) for kb in kbs]
                        unit(q, qb, 0, 128, TT, blocks, 128.0 ** -0.5, 128)
                        r, rb = recip_L(TT)
                        y, yb = self.ybp.get()
                        K.op(K.dve, lambda: nc.vector.tensor_tensor(out=y[:, :TT], in0=accO[:, :TT], in1=r[:, :TT], op=ALU.mult),
                             reads=[accOb, rb], writes=[yb])
                        store_y(y[:, :TT], yb, 3, h * 128, 128, a0, TT)

            NO = c.NO
            nob = NO // 128
            nsk = 2 + nob + 2
            kTs = self.sb(es, "t_kTs", [128, nsk * 128], BF16); kTsb = Buf()
            vs = self.sb(es, "t_vs", [128, nsk, 64], BF16); vsb = Buf()
            kvs3 = lambda r0, n: self.dram["kvTM"][r0:r0 + n * 128, :].rearrange("(b p) c -> p b c", p=128)
            for g in range(2):
                for half in range(2):
                    K.dma_in(kTs[half * 64:(half + 1) * 64, 0:(2 + nob) * 128],
                             self.dram["skT"][g * 64:(g + 1) * 64, 0:(2 + nob) * 128], reads=[self.dbuf["skT"]], writes=[kTsb])
                    K.dma_in(kTs[half * 64:(half + 1) * 64, (2 + nob) * 128:nsk * 128],
                             self.dram["skT"][g * 64:(g + 1) * 64, c.HALO0:c.HALO0 + 256], reads=[self.dbuf["skT"]], writes=[kTsb])
                for b0 in range(0, 2 + nob, 16):
                    b1 = min(2 + nob, b0 + 16)
                    K.dma_in(vs[:, b0:b1, :], kvs3(0, 2 + nob)[:, b0:b1, 1536 + g * 64:1536 + (g + 1) * 64],
                             reads=[self.dbuf["kvTM"]], writes=[vsb])
                K.dma_in(vs[:, 2 + nob:nsk, :], kvs3(c.HALO0, 2)[:, :, 1536 + g * 64:1536 + (g + 1) * 64],
                         reads=[self.dbuf["kvTM"]], writes=[vsb])
                for h in range(4 * g, 4 * g + 4):
                    p0 = (h % 2) * 64
                    for (a0, TT, kind) in qtiles:
                        q, qb = load_q("sqT", (h // 2) * 128, a0, TT)
                        bl = [(0, None), (1, None)]
                        if kind == "own":
                            i = (a0 - c.OWN0) // 512
                            for oi, o in enumerate(range(-1, 5)):
                                ob_ = 4 * i + o
                                if ob_ < 0:
                                    bl.append((2 + nob, 6))
                                elif ob_ >= nob:
                                    bl.append((2 + nob + 1, 7))
                                else:
                                    bl.append((2 + ob_, oi))
                        blocks = [(kTs[p0:p0 + 64, b * 128:(b + 1) * 128], vs[:, b, :], None if mi is None else sm[:, mi, :],
                                   [kTsb, vsb]) for (b, mi) in bl]
                        unit(q, qb, p0, p0 + 64, TT, blocks, 0.125, 64)
                        r, rb = recip_L(TT, add_ap=esk[:, h:h + 1])
                        y, yb = self.ybp.get()
                        K.op(K.dve, lambda: nc.vector.tensor_tensor(out=y[0:64, :TT], in0=accO[0:64, :TT], in1=r[0:64, :TT], op=ALU.mult),
                             reads=[accOb, rb], writes=[yb])
                        store_y(y[0:64, :TT], yb, 2, h * 64, 64, a0, TT)

    Prog.phase_attn = phase_attn


_attn_methods()


def _ffn_methods():
    def phase_merge_ffn(self):
        K, nc, c = self.K, self.nc, self.cfg
        K.barrier()
        with ExitStack() as es:
            self.psum = RPool(K, es, "psf", 8, [128, 512], F32, psum=True)
            self.tfp = RPool(K, es, "f_tf", 4, [128, 512], F32)
            self.rsp = RPool(K, es, "f_rs", 2, [128, 512], F32)
            wpool = RPool(K, es, "f_w", 2, [128, KC * 512], BF16)
            xt = self.sb(es, "f_xt", [128, KC, 512], F32); xtb = Buf()
            hT = self.sb(es, "f_hT", [128, KC, 512], BF16); hTb = Buf()
            ysb = self.sb(es, "f_y", [128, KC, 512], BF16); ysbb = Buf()
            accT = self.sb(es, "f_acc", [128, KC, 512], BF16); accTb = Buf()
            aT = self.sb(es, "f_aT", [128, 44, 512], BF16); aTb = Buf()
            accf = self.sb(es, "f_accf", [128, 4, 512], F32); accfb = [Buf() for _ in range(4)]
            sg = self.sb(es, "f_sg", [128, 4, 512], F32); sgb = [Buf() for _ in range(4)]
            fng = self.sb(es, "f_fng", [128, KC], F32)
            K.dma_in(fng[:], self.dram["fng"], writes=[self.cb])
            x3 = self.dram["xT"].rearrange("(k p) a -> p k a", p=128)
            h3 = self.dram["hT"].rearrange("(k p) a -> p k a", p=128)
            y3 = self.dram["yT"].rearrange("i (k p) a -> p (i k) a", p=128)
            qtiles = [t for t in c.qtiles if not (c.last and t[2] == "ctx")]
            for (a0, TT, kind) in qtiles:
                v = 1 if kind == "ctx" else 0
                K.dma_in(xt[:, :, :TT], x3[:, :, a0:a0 + TT], writes=[xtb])
                K.dma_in(hT[:, :, :TT], h3[:, :, a0:a0 + TT], reads=[self.dbuf["hT"]], writes=[hTb])
                K.dma_in(ysb[:, :, :TT], y3[:, :, a0:a0 + TT], reads=[self.dbuf["yT"]], writes=[ysbb])
                for cbk in range(4):
                    for i in range(4):
                        def cb_gate(blk, j, ps, pb):
                            K.op(K.act, lambda: nc.scalar.activation(out=sg[:, j, :TT], in_=ps[:, :TT], func=AF.Sigmoid),
                                 reads=[pb], writes=[sgb[j]])

                        def cb_proj(blk, j, ps, pb, i=i):
                            if i == 0:
                                K.op(K.dve, lambda: nc.vector.tensor_tensor(out=accf[:, j, :TT], in0=ps[:, :TT], in1=sg[:, j, :TT],
                                                                            op=ALU.mult), reads=[pb, sgb[j]], writes=[accfb[j]])
                            else:
                                t, tb = self.tfp.get()
                                K.op(K.dve, lambda: nc.vector.tensor_tensor(out=t[:, :TT], in0=ps[:, :TT], in1=sg[:, j, :TT],
                                                                            op=ALU.mult), reads=[pb, sgb[j]], writes=[tb])
                                K.op(K.pool, lambda: nc.gpsimd.tensor_tensor(out=accf[:, j, :TT], in0=accf[:, j, :TT], in1=t[:, :TT],
                                                                             op=ALU.add), reads=[tb, accfb[j]], writes=[accfb[j]])

                        self.fm_gemm(lambda k: hT[:, k, :TT], [hTb], TT, "wG", [i * 4 + cbk], 1, KC, cb_gate, wpool)
                        wsm = RPoolView(wpool, 4 * 512)
                        self.fm_gemm(lambda k, i=i: ysb[:, i * 4 + k, :TT], [ysbb], TT, "wBr", [i * 4 + cbk], 1, 4, cb_proj, wpool)
                    for j in range(4):
                        K.op(K.act, lambda: nc.scalar.copy(accT[:, cbk * 4 + j, :TT], accf[:, j, :TT]), reads=[accfb[j]], writes=[accTb])

                def cb_out(blk, j, ps, pb, v=v):
                    cc = blk * 4 + j
                    K.op(K.dve, lambda: nc.vector.scalar_tensor_tensor(out=xt[:, cc, :TT], in0=ps[:, :TT], scalar=self.G1[:, cc, v:v + 1],
                                                                       in1=xt[:, cc, :TT], op0=ALU.mult, op1=ALU.add),
                         reads=[pb, xtb, self.cb], writes=[xtb])

                self.fm_gemm(lambda k: accT[:, k, :TT], [accTb], TT, "wO", range(4), 1, KC, cb_out, wpool)
                self.norm_mod(es, xt, xtb, ysb, ysbb, hT, hTb, TT, self.A2, self.B2, v)
                for bk in range(11):
                    def cb_g(blk, j, ps, pb):
                        K.op(K.act, lambda: nc.scalar.activation(out=sg[:, j, :TT], in_=ps[:, :TT], func=AF.Silu),
                             reads=[pb], writes=[sgb[j]])

                    def cb_u(blk, j, ps, pb, bk=bk):
                        K.op(K.dve, lambda: nc.vector.tensor_tensor(out=aT[:, bk * 4 + j, :TT], in0=ps[:, :TT], in1=sg[:, j, :TT],
                                                                    op=ALU.mult), reads=[pb, sgb[j]], writes=[aTb])

                    self.fm_gemm(lambda k: hT[:, k, :TT], [hTb], TT, "wU", [bk], 1, KC, cb_g, wpool)
                    self.fm_gemm(lambda k: hT[:, k, :TT], [hTb], TT, "wU", [11 + bk], 1, KC, cb_u, wpool)

                def cb_down(blk, j, ps, pb, v=v):
                    cc = blk * 4 + j
                    K.op(K.dve, lambda: nc.vector.scalar_tensor_tensor(out=xt[:, cc, :TT], in0=ps[:, :TT], scalar=self.G2[:, cc, v:v + 1],
                                                                       in1=xt[:, cc, :TT], op0=ALU.mult, op1=ALU.add),
                         reads=[pb, xtb, self.cb], writes=[xtb])

                self.fm_gemm(lambda k: aT[:, k, :TT], [aTb], TT, "wD", range(4), 4, 11, cb_down, wpool)
                if c.last:
                    K.op(K.act, lambda: nc.scalar.activation(out=ysb[:, :, :TT], in_=xt[:, :, :TT], func=AF.Square),
                         reads=[xtb], writes=[ysbb])
                    ssp, sspb = self.psum.get()
                    K.mm(ssp[:, :TT], [(self.ones_bf[:], ysb[:, kc, :TT]) for kc in range(KC)], reads=[ysbb, self.cb], wbuf=sspb)
                    r1, r1b = self.rstd_from_ps(ssp, sspb, TT, float(D))
                    for kc in range(KC):
                        K.op(K.dve, lambda: nc.vector.scalar_tensor_tensor(out=xt[:, kc, :TT], in0=xt[:, kc, :TT], scalar=fng[:, kc:kc + 1],
                                                                           in1=r1[:, :TT], op0=ALU.mult, op1=ALU.mult),
                             reads=[xtb, r1b, self.cb], writes=[xtb])
                o3 = self.dram["outT"].rearrange("(k p) a -> p k a", p=128)
                if kind == "ctx":
                    c3 = self.dram["ctxoT"].rearrange("(k p) a -> p k a", p=128)
                    K.dma_out(c3[:, :, 0:TT], xt[:, :, :TT], reads=[xtb], writes=[self.dbuf["ctxoT"]])
                else:
                    K.dma_out(o3[:, :, a0 - c.OWN0:a0 - c.OWN0 + TT], xt[:, :, :TT], reads=[xtb], writes=[self.dbuf["outT"]])

    Prog.phase_merge_ffn = phase_merge_ffn


def RPoolView(pool, n):
    return pool


_ffn_methods()


def _mlstm_methods():
    def phase_mlstm(self):
        K, nc, c = self.K, self.nc, self.cfg
        K.barrier()
        C = c.NKV // 128
        NOc = c.NO // 128
        NOTc = c.NOT // 128
        OTH = 2 + NOc
        with ExitStack() as es:
            self.psum = RPool(K, es, "psl", 8, [128, 512], F32, psum=True)
            self.rsp = RPool(K, es, "l_rs", 2, [128, 512], F32)
            gb = Buf("gates")
            gt = self.sb(es, "l_gt", [128, C, 16], F32)
            g3 = self.dram["gatesTM"][0:c.NKV, :].rearrange("(c p) j -> p c j", p=128)
            for b0 in range(0, C, 16):
                b1 = min(C, b0 + 16)
                K.dma_in(gt[:, b0:b1, :], g3[:, b0:b1, :], reads=[self.dbuf["gatesTM"]], writes=[gb])
            gt4 = gt[:].rearrange("p c (d k) -> p c d k", d=2)
            sp = self.sb(es, "l_sp", [128, C, 2, 4], F32)
            cs = self.sb(es, "l_cs", [128, C, 2, 4], F32)
            tot = self.sb(es, "l_tot", [128, C, 2, 4], F32)
            A = self.sb(es, "l_A", [128, C, 2, 4], F32)
            E = self.sb(es, "l_E", [128, C, 2, 4], F32)
            DEC = self.sb(es, "l_DEC", [128, C, 2, 4], F32)
            W = self.sb(es, "l_W", [128, C, 2, 4], F32)
            tm = self.sb(es, "l_tm", [128, 2, 128], F32)
            mf = self.sb(es, "l_mf", [128, 2, NOTc], F32)
            mng = self.sb(es, "l_mng", [128, 4], F32)
            K.dma_in(tm[:], self.dram["trimask"], writes=[gb])
            K.dma_in(mf[:], self.dram["mflags"], writes=[gb])
            K.dma_in(mng[:], self.dram["mng"], writes=[gb])
            K.op(K.act, lambda: nc.scalar.activation(out=sp[:], in_=gt4[:, :, :, 4:8], func=AF.Exp, scale=-1.0), reads=[gb], writes=[gb])
            K.op(K.act, lambda: nc.scalar.activation(out=sp[:], in_=sp[:], func=AF.Ln, bias=self.ones_f[:, 0:1], scale=1.0),
                 reads=[gb, self.cb], writes=[gb])
            for c0 in range(0, C, 64):
                cc = min(64, C - c0)
                for d in range(2):
                    ps, pb = self.psum.get()
                    K.mm(ps[:, 0:cc * 4].rearrange("p (c k) -> p c k", k=4), [(tm[:, d, :], sp[:, c0:c0 + cc, d, :])], reads=[gb], wbuf=pb)
                    K.op(K.dve, lambda: nc.vector.tensor_copy(cs[:, c0:c0 + cc, d, :], ps[:, 0:cc * 4].rearrange("p (c k) -> p c k", k=4)),
                         reads=[pb], writes=[gb])
                ps, pb = self.psum.get()
                K.mm(ps[:, 0:cc * 8], [(self.ones_f[:], sp[:, c0:c0 + cc].rearrange("p c d k -> p (c d k)"))], reads=[gb, self.cb], wbuf=pb)
                K.op(K.dve, lambda: nc.vector.tensor_copy(tot[:, c0:c0 + cc].rearrange("p c d k -> p (c d k)"), ps[:, 0:cc * 8]),
                     reads=[pb], writes=[gb])
            K.op(K.dve, lambda: nc.vector.tensor_tensor(out=A[:], in0=gt4[:, :, :, 0:4], in1=cs[:], op=ALU.add), reads=[gb], writes=[gb])
            K.op(K.act, lambda: nc.scalar.activation(out=A[:], in_=A[:], func=AF.Exp), reads=[gb], writes=[gb])
            K.op(K.act, lambda: nc.scalar.activation(out=E[:], in_=cs[:], func=AF.Exp), reads=[gb], writes=[gb])
            K.op(K.act, lambda: nc.scalar.activation(out=DEC[:], in_=tot[:], func=AF.Exp, scale=-1.0), reads=[gb], writes=[gb])
            K.op(K.dve, lambda: nc.vector.tensor_tensor(out=W[:], in0=A[:], in1=DEC[:], op=ALU.mult), reads=[gb], writes=[gb])
            for d in range(2):
                for h in range(4):
                    K.op(K.dve, lambda: nc.vector.tensor_tensor(out=W[:, OTH:C, d, h], in0=W[:, OTH:C, d, h], in1=mf[:, d, :], op=ALU.mult),
                         reads=[gb], writes=[gb])
                    K.op(K.dve, lambda: nc.vector.scalar_tensor_tensor(out=DEC[:, OTH:C, d, h], in0=DEC[:, OTH:C, d, h], scalar=-1.0,
                                                                       in1=mf[:, d, :], op0=ALU.add, op1=ALU.mult), reads=[gb], writes=[gb])
                    K.op(K.dve, lambda: nc.vector.tensor_scalar(out=DEC[:, OTH:C, d, h], in0=DEC[:, OTH:C, d, h], scalar1=1.0, scalar2=None,
                                                                op0=ALU.add), reads=[gb], writes=[gb])
            hf = self.sb(es, "l_hf", [128, 4, c.NOC], BF16); hfb = Buf()
            qcp = RPool(K, es, "l_qc", 3, [64, 2, 4, 128], BF16)
            sop = RPool(K, es, "l_so", 3, [128, 4, 128], BF16)
            mq3 = self.dram["mqT"].rearrange("(h p) a -> p h a", p=64)
            mk3 = self.dram["mkT"].rearrange("(h p) a -> p h a", p=64)
            so3 = self.dram["soT"].rearrange("(h p) a -> p h a", p=128)
            st = self.sb(es, "l_st", [64, 8, 256], F32)
            stb = self.sb(es, "l_stb", [64, 8, 256], BF16)
            stbuf = [Buf() for _ in range(8)]
            kp = RPool(K, es, "l_k", 3, [128, 256], BF16)
            vp = RPool(K, es, "l_v", 3, [128, 4, 256], BF16)
            for (t, b) in vp.tiles:
                K.op(K.pool, lambda: nc.gpsimd.memset(t[:, :, 128:256], 1.0), writes=[b])
            b16 = RPool(K, es, "l_b16", 6, [128, 256], BF16)
            f32p = RPool(K, es, "l_f32", 6, [128, 128], F32)
            ybp = RPool(K, es, "l_y", 3, [128, 128], BF16)

            def run_dir(d):
                for i in range(8):
                    if i // 4 == d:
                        K.op(K.dve, lambda: nc.vector.memset(st[:, i, :], 0.0), writes=[stbuf[i]])
                        K.op(K.pool, lambda: nc.gpsimd.memset(stb[:, i, :], 0.0), writes=[stbuf[i]])
                others = [OTH + j for j in range(NOTc)]
                own = [2 + j for j in range(NOc)]
                if d == 0:
                    seq = [(0, not c.last), (1, not c.last)] + [(x, False) for x in others] + [(x, True) for x in own]
                else:
                    seq = [(1, not c.last), (0, not c.last)] + [(x, False) for x in reversed(others)] + [(x, True) for x in reversed(own)]
                for si, (ci, full) in enumerate(seq):
                    kch, kchb = kp.get()
                    va, vab = vp.get()
                    K.dma_in(kch[:], self.dram["kvTM"][ci * 128:(ci + 1) * 128, 0:256], reads=[self.dbuf["kvTM"]], writes=[kchb])
                    K.dma_in(va[:, :, 0:128], self.dram["kvTM"][ci * 128:(ci + 1) * 128, 512:1024].rearrange("p (h e) -> p h e", h=4),
                             reads=[self.dbuf["kvTM"]], writes=[vab])
                    col = ci * 128
                    if full:
                        qc, qb = qcp.get()
                        K.dma_in(qc[:, 0], mq3[:, :, col:col + 128], reads=[self.dbuf["mqT"]], writes=[qb])
                        K.dma_in(qc[:, 1], mk3[:, :, col:col + 128], reads=[self.dbuf["mkT"]], writes=[qb])
                        if d == 1:
                            soc, socb = sop.get()
                            K.dma_in(soc[:], so3[:, :, col:col + 128], reads=[self.dbuf["soT"]], writes=[socb])
                    for h in range(4):
                        sidx = d * 4 + h
                        sb_ = stbuf[sidx]
                        if full:
                            S, Sb = self.psum.get()
                            K.mm(S[:, 0:128], [(qc[:, 1, h, :], qc[:, 0, h, :])], reads=[qb], wbuf=Sb)
                            Sm, Smb = b16.get()
                            K.op(K.dve, lambda: nc.vector.tensor_tensor(out=Sm[:, 0:128], in0=S[:, 0:128], in1=tm[:, d, :], op=ALU.mult),
                                 reads=[Sb, gb], writes=[Smb])
                            vA, vAb = b16.get()
                            K.op(K.dve, lambda: nc.vector.tensor_scalar(out=vA[:], in0=va[:, h, :], scalar1=A[:, ci, d, h:h + 1], scalar2=None,
                                                                        op0=ALU.mult), reads=[vab, gb], writes=[vAb])
                            num, numb = self.psum.get()
                            K.mm(num[:, 0:128], [(vA[:, 0:128], Sm[:, 0:128]), (stb[:, sidx, 0:128], qc[:, 0, h, :])],
                                 reads=[vAb, Smb, sb_, qb], wbuf=numb)
                            den, denb = self.psum.get()
                            K.mm(den[:, 0:128], [(vA[:, 128:256], Sm[:, 0:128]), (stb[:, sidx, 128:256], qc[:, 0, h, :])],
                                 reads=[vAb, Smb, sb_, qb], wbuf=denb)
                            ie, ieb = b16.get()
                            K.op(K.dve, lambda: nc.vector.tensor_scalar(out=ie[:, 0:128], in0=self.ident_bf[:], scalar1=E[:, ci, d, h:h + 1],
                                                                        scalar2=None, op0=ALU.mult), reads=[gb, self.cb], writes=[ieb])
                            eb, ebb = self.psum.get()
                            K.mm(eb[:, 0:128], [(self.ones_bf[:], ie[:, 0:128])], reads=[ieb, self.cb], wbuf=ebb)
                            ebs, ebsb = f32p.get()
                            K.op(K.act, lambda: nc.scalar.copy(ebs[:], eb[:, 0:128]), reads=[ebb], writes=[ebsb])
                            mx, mxb = f32p.get()
                            K.op(K.act, lambda: nc.scalar.activation(out=mx[:], in_=den[:, 0:128], func=AF.Abs), reads=[denb], writes=[mxb])
                            K.op(K.dve, lambda: nc.vector.tensor_tensor(out=mx[:], in0=mx[:], in1=ebs[:], op=ALU.max),
                                 reads=[mxb, ebsb], writes=[mxb])
                            K.op(K.dve, lambda: nc.vector.reciprocal(out=mx[:], in_=mx[:]), reads=[mxb], writes=[mxb])
                            if d == 0:
                                K.op(K.dve, lambda: nc.vector.tensor_tensor(out=hf[:, h, col:col + 128], in0=num[:, 0:128], in1=mx[:], op=ALU.mult),
                                     reads=[numb, mxb], writes=[hfb])
                            else:
                                hs, hsb = f32p.get()
                                K.op(K.dve, lambda: nc.vector.tensor_tensor(out=hs[:], in0=num[:, 0:128], in1=mx[:], op=ALU.mult),
                                     reads=[numb, mxb], writes=[hsb])
                                K.op(K.pool, lambda: nc.gpsimd.tensor_tensor(out=hs[:], in0=hs[:], in1=hf[:, h, col:col + 128], op=ALU.add),
                                     reads=[hsb, hfb], writes=[hsb])
                                sq, sqb = b16.get()
                                K.op(K.act, lambda: nc.scalar.activation(out=sq[:, 0:128], in_=hs[:], func=AF.Square), reads=[hsb], writes=[sqb])
                                ssp, sspb = self.psum.get()
                                K.mm(ssp[:, 0:128], [(self.ones_bf[:], sq[:, 0:128])], reads=[sqb, self.cb], wbuf=sspb)
                                r1, r1b = self.rstd_from_ps(ssp, sspb, 128, 128.0)
                                K.op(K.dve, lambda: nc.vector.scalar_tensor_tensor(out=hs[:], in0=hs[:], scalar=mng[:, h:h + 1], in1=r1[:, 0:128],
                                                                                   op0=ALU.mult, op1=ALU.mult), reads=[hsb, r1b, gb], writes=[hsb])
                                y, yb = ybp.get()
                                K.op(K.dve, lambda: nc.vector.tensor_tensor(out=y[:], in0=hs[:], in1=soc[:, h, :], op=ALU.mult),
                                     reads=[hsb, socb], writes=[yb])
                                K.dma_out(self.dram["yT"][0, h * 128:(h + 1) * 128, col:col + 128], y[:], reads=[yb], writes=[self.dbuf["yT"]])
                        if si == len(seq) - 1:
                            continue
                        rU, rUb = b16.get()
                        K.op(K.dve, lambda: nc.vector.tensor_scalar(out=rU[:], in0=va[:, h, :], scalar1=W[:, ci, d, h:h + 1], scalar2=None,
                                                                    op0=ALU.mult), reads=[vab, gb], writes=[rUb])
                        U, Ub = self.psum.get()
                        K.mm(U[0:64, 0:256], [(kch[:, h * 64:(h + 1) * 64], rU[:])], reads=[kchb, rUb], wbuf=Ub)
                        K.op(K.dve, lambda: nc.vector.scalar_tensor_tensor(out=st[:, sidx, :], in0=st[:, sidx, :], scalar=DEC[0:64, ci, d, h:h + 1],
                                                                           in1=U[0:64, 0:256], op0=ALU.mult, op1=ALU.add),
                             reads=[Ub, gb, sb_], writes=[sb_])
                        K.op(K.act, lambda: nc.scalar.copy(stb[:, sidx, :], st[:, sidx, :]), reads=[sb_], writes=[sb_])

            run_dir(0)
            run_dir(1)

    Prog.phase_mlstm = phase_mlstm


_mlstm_methods()


def build_program(cfg):
    P = Prog(cfg)
    P.declare()
    with ExitStack() as es:
        P.consts(es)
        P.cast_weights()
        P.phase_mod()
        P.phase1a()
        P.phase1b()
        P.phase_mlstm()
        P.phase_attn()
        P.phase_merge_ffn()
        P.K.finish()
    return P


_PROGS = {}


def _get_prog(N, last, lam_init):
    key = (N, last)
    if key not in _PROGS:
        _PROGS[key] = build_program(Cfg(N=N, last=last, lam_init=lam_init))
    return _PROGS[key]


def kernel(**inputs):
    inputs = {k: np.asarray(v) for k, v in inputs.items()}
    x = np.asarray(inputs["x"], np.float32)
    B, N, _ = x.shape
    xc = np.asarray(inputs["ctx"], np.float32)
    NO = N // 4
    cur = x
    for l in range(DEPTH):
        last = l == DEPTH - 1
        lam_init = 0.8 - 0.6 * math.exp(-0.3 * l)
        P = _get_prog(N, last, lam_init)
        maps = [prep_core_inputs(inputs, l, cur, xc, b, r, P.cfg) for b in range(B) for r in range(4)]
        res = run_bass_kernel_spmd(P.nc, maps, core_ids=list(range(8)))
        nxt = np.empty_like(cur)
        for b in range(B):
            for r in range(4):
                nxt[b, r * NO:(r + 1) * NO] = np.asarray(res.results[b * 4 + r]["outT"]).T
        if not last:
            xc = np.stack([np.asarray(res.results[b * 4]["ctxoT"]).T for b in range(B)], 0).astype(np.float32)
        cur = nxt
    return cur.astype(np.float32)
```
